# Optimizing a Trainium2 kernel written in Bass

```python
import jax, jax.numpy as jnp
from jax import lax
import numpy as np

D_MODEL = 1024
BATCH = 8
SEQ = 2048
DEPTH = 2

CTX_LEN = 256
GRID_W = 64
N_MOD = 9
D_FF = 2816
EPS = 1e-6
A_HEADS = 4
A_HEAD_DIM = 128
A_WIDTH = A_HEADS * A_HEAD_DIM
MLSTM_CHUNK = 128
CONV_W = 3
B_GROUPS = 4
B_GROUP_DIM = 128
B_WIDTH = B_GROUPS * B_GROUP_DIM
SGU_CHUNK = 128
EVEN_IN = 4 * A_WIDTH + 4 * A_HEADS + 2 * B_WIDTH
EVEN_MIX = A_WIDTH + B_WIDTH
C_HEADS = 16
C_KV_HEADS = 4
C_GROUP = C_HEADS // C_KV_HEADS
C_HEAD_DIM = 64
WINDOW = 128
ATTN_BLOCK = 128
ODD_QKV = (C_HEADS + 2 * C_KV_HEADS) * C_HEAD_DIM
ROPE_BASE = 10000.0
N_EVEN = (DEPTH + 1) // 2
N_ODD = DEPTH // 2

kernel_name = 'hybrid_mlstm_sgu_swa_dit_block'

f32 = jnp.float32


def _rms_norm(t):
    tf = t.astype(f32)
    return (tf * lax.rsqrt(jnp.mean(tf * tf, axis=-1, keepdims=True) + EPS)).astype(t.dtype)


def _modulation(s, w, b):
    m = s @ w + b
    return m.reshape(s.shape[0], N_MOD, 1, D_MODEL)


def _modulate(h, shift, scale):
    return _rms_norm(h) * (1.0 + scale) + shift


def _swiglu(h, w_in, w_out):
    g, u = jnp.split(h @ w_in, 2, axis=-1)
    return (jax.nn.silu(g) * u) @ w_out


def _axial_rope_tables(T):
    rows = T // GRID_W
    row, col = jnp.meshgrid(jnp.arange(rows), jnp.arange(GRID_W), indexing='ij')
    n_freq = C_HEAD_DIM // 4
    inv = ROPE_BASE ** (-jnp.arange(n_freq, dtype=f32) / n_freq)
    ang = jnp.concatenate([row.reshape(-1, 1).astype(f32) * inv,
                           col.reshape(-1, 1).astype(f32) * inv], axis=-1)
    return jnp.cos(ang), jnp.sin(ang)


def _apply_rope(t, cos, sin):
    x1, x2 = t[..., 0::2], t[..., 1::2]
    cs = cos[None, :, None, :].astype(t.dtype)
    sn = sin[None, :, None, :].astype(t.dtype)
    return jnp.stack([x1 * cs - x2 * sn, x1 * sn + x2 * cs], axis=-1).reshape(t.shape)


def _centred_depthwise_conv(x, w):
    C = x.shape[-1]
    return lax.conv_general_dilated(x, w[:, None, :].astype(x.dtype), window_strides=(1,),
                                    padding=[(CONV_W // 2, CONV_W // 2)],
                                    dimension_numbers=('NWC', 'WIO', 'NWC'),
                                    feature_group_count=C)


def _zero_state(B_):
    return (jnp.zeros((B_, A_HEADS, A_HEAD_DIM, A_HEAD_DIM), f32),
            jnp.zeros((B_, A_HEADS, A_HEAD_DIM), f32),
            jnp.zeros((B_, A_HEADS), f32))


def _mlstm_chunkwise(q, k, v, ig, lf, state):
    B_, H, T, d = q.shape
    L = MLSTM_CHUNK
    N = T // L
    q = q.reshape(B_, H, N, L, d)
    k = k.reshape(B_, H, N, L, d)
    v = v.reshape(B_, H, N, L, d)
    ig = ig.reshape(B_, H, N, L)
    b = jnp.cumsum(lf.reshape(B_, H, N, L), axis=-1)
    g = b[..., -1]
    a = g[..., None] - b + ig
    m_loc = jnp.max(a, axis=-1)
    w = jnp.exp(a - m_loc[..., None])
    C_loc = jnp.einsum('bhnl,bhnld,bhnle->bhnde', w, k, v)
    n_loc = jnp.einsum('bhnl,bhnld->bhnd', w, k)

    def step(carry, xs):
        C, n, m = carry
        g_j, m_loc_j, C_loc_j, n_loc_j = xs
        m_new = jnp.maximum(g_j + m, m_loc_j)
        dec = jnp.exp(g_j + m - m_new)
        add = jnp.exp(m_loc_j - m_new)
        C_new = dec[..., None, None] * C + add[..., None, None] * C_loc_j
        n_new = dec[..., None] * n + add[..., None] * n_loc_j
        return (C_new, n_new, m_new), (C, n, m)

    xs = (jnp.moveaxis(g, 2, 0), jnp.moveaxis(m_loc, 2, 0),
          jnp.moveaxis(C_loc, 2, 0), jnp.moveaxis(n_loc, 2, 0))
    final, (C_prev, n_prev, m_prev) = lax.scan(step, state, xs)
    C_prev = jnp.moveaxis(C_prev, 0, 2)
    n_prev = jnp.moveaxis(n_prev, 0, 2)
    m_prev = jnp.moveaxis(m_prev, 0, 2)

    e = b + m_prev[..., None]
    tril = jnp.tril(jnp.ones((L, L), dtype=bool))
    Dm = jnp.where(tril, b[..., :, None] - b[..., None, :] + ig[..., None, :], -jnp.inf)
    m_t = jnp.maximum(e, jnp.max(Dm, axis=-1))
    S = jnp.einsum('bhntd,bhnsd->bhnts', q, k) * jnp.exp(Dm - m_t[..., None])
    inter = jnp.exp(e - m_t)
    num = (jnp.einsum('bhnts,bhnse->bhnte', S, v)
           + inter[..., None] * jnp.einsum('bhntd,bhnde->bhnte', q, C_prev))
    den = jnp.sum(S, axis=-1) + inter * jnp.einsum('bhntd,bhnd->bhnt', q, n_prev)
    h = num / jnp.maximum(jnp.abs(den), jnp.exp(-m_t))[..., None]
    return h.reshape(B_, H, T, d), final


def _flip(ts):
    return tuple(jnp.flip(t, axis=2) for t in ts)


def _even_stream_inputs(n, w_in, conv_w, gate_b):
    B_, T, _ = n.shape
    p = n @ w_in
    qk, v, o, gates, uv = jnp.split(
        p, [2 * A_WIDTH, 3 * A_WIDTH, 4 * A_WIDTH, 4 * A_WIDTH + 4 * A_HEADS], axis=-1)
    q, k = jnp.split(jax.nn.silu(_centred_depthwise_conv(qk, conv_w)), 2, axis=-1)

    def heads(t):
        return t.reshape(B_, T, A_HEADS, A_HEAD_DIM).transpose(0, 2, 1, 3).astype(f32)

    q, k, v = heads(q), heads(k) * (A_HEAD_DIM ** -0.5), heads(v)
    gt = (gates.astype(f32) + gate_b.reshape(-1).astype(f32)).reshape(B_, T, 4, A_HEADS).transpose(2, 0, 3, 1)
    fwd = (gt[0], jax.nn.log_sigmoid(gt[1]))
    bwd = (gt[2], jax.nn.log_sigmoid(gt[3]))
    return (q, k, v), fwd, bwd, o, uv


def _spatial_gating(uv, norm_g, ws, sb):
    u, v = jnp.split(jax.nn.gelu(uv), 2, axis=-1)
    v = _rms_norm(v) * norm_g
    B_, T, _ = v.shape
    vb = v.reshape(B_, T // SGU_CHUNK, SGU_CHUNK, B_GROUPS, B_GROUP_DIM)
    mixed = jnp.einsum('gpq,bnqgc->bnpgc', ws, vb) + sb.T[:, :, None]
    return u * mixed.reshape(B_, T, B_WIDTH)


def _even_output(hsum, o, uv, mnorm, sgu_g, ws, sb, w_out):
    B_, H, T, d = hsum.shape
    hn = hsum * lax.rsqrt(jnp.mean(hsum * hsum, axis=-1, keepdims=True) + EPS) * mnorm[None, :, None, :].astype(f32)
    h_a = jax.nn.sigmoid(o) * hn.transpose(0, 2, 1, 3).reshape(B_, T, A_WIDTH).astype(o.dtype)
    h_b = _spatial_gating(uv, sgu_g, ws, sb)
    return jnp.concatenate([h_a, h_b], axis=-1) @ w_out


def _even_mixer(nx, nc, w_in, w_out, conv_w, gate_b, mnorm, sgu_g, ws, sb, need_ctx):
    qkv_c, fw_c, bw_c, o_c, uv_c = _even_stream_inputs(nc, w_in, conv_w, gate_b)
    qkv_x, fw_x, bw_x, o_x, uv_x = _even_stream_inputs(nx, w_in, conv_w, gate_b)
    zero = _zero_state(nx.shape[0])
    hc_f, st_f = _mlstm_chunkwise(*qkv_c, *fw_c, zero)
    hc_b, st_b = _mlstm_chunkwise(*_flip(qkv_c), *_flip(bw_c), zero)
    hx_f, _ = _mlstm_chunkwise(*qkv_x, *fw_x, st_f)
    hx_b, _ = _mlstm_chunkwise(*_flip(qkv_x), *_flip(bw_x), st_b)
    yx = _even_output(hx_f + jnp.flip(hx_b, axis=2), o_x, uv_x, mnorm, sgu_g, ws, sb, w_out)
    yc = None
    if need_ctx:
        yc = _even_output(hc_f + jnp.flip(hc_b, axis=2), o_c, uv_c, mnorm, sgu_g, ws, sb, w_out)
    return yx, yc


def _softmax_with_sink(sink, *scores):
    ref = scores[0]
    s0 = jnp.broadcast_to(sink[None, :, :, None, None].astype(f32), ref.shape[:-1] + (1,))
    p = jax.nn.softmax(jnp.concatenate([s0] + [s.astype(f32) for s in scores], axis=-1), axis=-1)
    parts = []
    off = 1
    for s in scores:
        parts.append(p[..., off:off + s.shape[-1]])
        off += s.shape[-1]
    return parts


def _window_attention(q, k, v, kc, vc, sink):
    B_, T, _, _ = q.shape
    nb = T // ATTN_BLOCK
    scale = C_HEAD_DIM ** -0.5
    qb = (q * scale).reshape(B_, nb, ATTN_BLOCK, C_KV_HEADS, C_GROUP, C_HEAD_DIM).transpose(1, 0, 2, 3, 4, 5)
    pad = ((0, 0), (ATTN_BLOCK, ATTN_BLOCK), (0, 0), (0, 0))
    kp, vp = jnp.pad(k, pad), jnp.pad(v, pad)
    offs = jnp.arange(ATTN_BLOCK)
    band_offs = jnp.arange(3 * ATTN_BLOCK) - ATTN_BLOCK

    def block(args):
        qj, j = args
        start = j * ATTN_BLOCK
        kj = lax.dynamic_slice_in_dim(kp, start, 3 * ATTN_BLOCK, axis=1)
        vj = lax.dynamic_slice_in_dim(vp, start, 3 * ATTN_BLOCK, axis=1)
        qpos = start + offs
        kpos = start + band_offs
        valid = (jnp.abs(qpos[:, None] - kpos[None, :]) <= WINDOW) & (kpos[None, :] >= 0) & (kpos[None, :] < T)
        s_band = jnp.where(valid, jnp.einsum('bqhgd,bkhd->bhgqk', qj, kj).astype(f32), -jnp.inf)
        s_ctx = jnp.einsum('bqhgd,bkhd->bhgqk', qj, kc)
        p_ctx, p_band = _softmax_with_sink(sink, s_ctx, s_band)
        return (jnp.einsum('bhgqk,bkhd->bqhgd', p_ctx.astype(vc.dtype), vc)
                + jnp.einsum('bhgqk,bkhd->bqhgd', p_band.astype(vj.dtype), vj))

    out = lax.map(block, (qb, jnp.arange(nb)))
    return out.transpose(1, 0, 2, 3, 4, 5).reshape(B_, T, C_HEADS * C_HEAD_DIM)


def _context_attention(qc, kc, vc, sink):
    B_, Tc = qc.shape[:2]
    q = (qc * (C_HEAD_DIM ** -0.5)).reshape(B_, Tc, C_KV_HEADS, C_GROUP, C_HEAD_DIM)
    (p,) = _softmax_with_sink(sink, jnp.einsum('bqhgd,bkhd->bhgqk', q, kc))
    return jnp.einsum('bhgqk,bkhd->bqhgd', p.astype(vc.dtype), vc).reshape(B_, Tc, C_HEADS * C_HEAD_DIM)


def _odd_mixer(nx, nc, w_qkv, w_out, sink, cos, sin, need_ctx):
    B_, T, _ = nx.shape
    Tc = nc.shape[1]
    qdim = C_HEADS * C_HEAD_DIM
    kvdim = C_KV_HEADS * C_HEAD_DIM
    q, k, v = jnp.split(nx @ w_qkv, [qdim, qdim + kvdim], axis=-1)
    q = _apply_rope(q.reshape(B_, T, C_HEADS, C_HEAD_DIM), cos, sin)
    k = _apply_rope(k.reshape(B_, T, C_KV_HEADS, C_HEAD_DIM), cos, sin)
    v = v.reshape(B_, T, C_KV_HEADS, C_HEAD_DIM)
    kc, vc = jnp.split(nc @ w_qkv[:, qdim:], 2, axis=-1)
    kc = kc.reshape(B_, Tc, C_KV_HEADS, C_HEAD_DIM)
    vc = vc.reshape(B_, Tc, C_KV_HEADS, C_HEAD_DIM)
    sink = sink.reshape(C_KV_HEADS, C_GROUP)
    yx = _window_attention(q, k, v, kc, vc, sink) @ w_out
    yc = None
    if need_ctx:
        qc = (nc @ w_qkv[:, :qdim]).reshape(B_, Tc, C_HEADS, C_HEAD_DIM)
        yc = _context_attention(qc, kc, vc, sink) @ w_out
    return yx, yc


def setup_inputs(seed: int = 0) -> dict:
    key = jax.random.key(seed)
    ks = jax.random.split(key, 20)
    D = D_MODEL

    def nrm(k, shape, scale):
        return jax.random.normal(k, shape, jnp.float32) * scale

    is_forget = jnp.array([0.0, 1.0, 0.0, 1.0], jnp.float32)[:, None]
    forget_lin = jnp.linspace(3.0, 6.0, A_HEADS, dtype=jnp.float32)[None, :]
    return {
        'x': nrm(ks[0], (BATCH, SEQ, D), 1.0),
        'c': nrm(ks[1], (BATCH, D), 1.0),
        'ctx': nrm(ks[2], (BATCH, CTX_LEN, D), 1.0),
        'c_ctx': nrm(ks[3], (D,), 1.0),
        'ada_w': nrm(ks[4], (DEPTH, D, N_MOD * D), 0.5 * D ** -0.5),
        'ada_b': nrm(ks[5], (DEPTH, N_MOD * D), 0.02),
        'ffn_w_in': nrm(ks[6], (DEPTH, 2, D, 2 * D_FF), D ** -0.5),
        'ffn_w_out': nrm(ks[7], (DEPTH, 2, D_FF, D), D_FF ** -0.5),
        'even_w_in': nrm(ks[8], (N_EVEN, D, EVEN_IN), D ** -0.5),
        'even_w_out': nrm(ks[9], (N_EVEN, EVEN_MIX, D), EVEN_MIX ** -0.5),
        'mlstm_conv': nrm(ks[10], (N_EVEN, CONV_W, 2 * A_WIDTH), CONV_W ** -0.5),
        'mlstm_gate_b': is_forget * forget_lin + nrm(ks[11], (N_EVEN, 4, A_HEADS), 0.1),
        'mlstm_norm': 1.0 + nrm(ks[12], (N_EVEN, A_HEADS, A_HEAD_DIM), 0.05),
        'sgu_norm': 1.0 + nrm(ks[13], (N_EVEN, B_WIDTH), 0.05),
        'sgu_ws': nrm(ks[14], (N_EVEN, B_GROUPS, SGU_CHUNK, SGU_CHUNK), SGU_CHUNK ** -0.5),
        'sgu_b': 1.0 + nrm(ks[15], (N_EVEN, B_GROUPS, SGU_CHUNK), 0.1),
        'odd_w_qkv': nrm(ks[16], (N_ODD, D, ODD_QKV), D ** -0.5),
        'odd_w_out': nrm(ks[17], (N_ODD, C_HEADS * C_HEAD_DIM, D), (C_HEADS * C_HEAD_DIM) ** -0.5),
        'attn_sink': nrm(ks[18], (N_ODD, C_HEADS), 0.5),
        'final_norm': 1.0 + nrm(ks[19], (D,), 0.05),
    }


def reference(x, c, ctx, c_ctx, ada_w, ada_b, ffn_w_in, ffn_w_out, even_w_in, even_w_out,
              mlstm_conv, mlstm_gate_b, mlstm_norm, sgu_norm, sgu_ws, sgu_b,
              odd_w_qkv, odd_w_out, attn_sink, final_norm):
    cos, sin = _axial_rope_tables(x.shape[1])
    sc = jax.nn.silu(c)
    scc = jax.nn.silu(c_ctx)[None]
    h, hc = x, ctx
    for layer in range(DEPTH):
        last = layer == DEPTH - 1
        mod = _modulation(sc, ada_w[layer], ada_b[layer])
        modc = _modulation(scc, ada_w[layer], ada_b[layer])
        h = h + 0.5 * mod[:, 2] * _swiglu(_modulate(h, mod[:, 0], mod[:, 1]),
                                          ffn_w_in[layer, 0], ffn_w_out[layer, 0])
        hc = hc + 0.5 * modc[:, 2] * _swiglu(_modulate(hc, modc[:, 0], modc[:, 1]),
                                             ffn_w_in[layer, 0], ffn_w_out[layer, 0])
        nx = _modulate(h, mod[:, 3], mod[:, 4])
        nc = _modulate(hc, modc[:, 3], modc[:, 4])
        if layer % 2 == 0:
            e = layer // 2
            yx, yc = _even_mixer(nx, nc, even_w_in[e], even_w_out[e], mlstm_conv[e], mlstm_gate_b[e],
                                 mlstm_norm[e], sgu_norm[e], sgu_ws[e], sgu_b[e], not last)
        else:
            o = layer // 2
            yx, yc = _odd_mixer(nx, nc, odd_w_qkv[o], odd_w_out[o], attn_sink[o], cos, sin, not last)
        h = h + mod[:, 5] * yx
        h = h + 0.5 * mod[:, 8] * _swiglu(_modulate(h, mod[:, 6], mod[:, 7]),
                                          ffn_w_in[layer, 1], ffn_w_out[layer, 1])
        if not last:
            hc = hc + modc[:, 5] * yc
            hc = hc + 0.5 * modc[:, 8] * _swiglu(_modulate(hc, modc[:, 6], modc[:, 7]),
                                                 ffn_w_in[layer, 1], ffn_w_out[layer, 1])
    return _rms_norm(h) * final_norm
```

```python
import numpy as np
import concourse.bass as bass
import concourse.mybir as mybir
from concourse.bass_utils import run_bass_kernel_spmd

F32 = mybir.dt.float32
BF16 = mybir.dt.bfloat16
AF = mybir.ActivationFunctionType
ALU = mybir.AluOpType
AX = mybir.AxisListType

COMPUTE = ("tensor", "vector", "scalar", "gpsimd")

D = 1024
KC = 8
TCX = 256
TX = 2048
T = 2304
NTILE = 18
DFF = 2816
NFC = 22
EPS = 1e-6
ST_ALL = [(0, 256), (256, 768), (768, 1280), (1280, 1792), (1792, 2304)]
ST_X = ST_ALL[1:]
ARENA = 93 * 1024


class Op:
    __slots__ = ("eng", "fn", "deps", "is_dma", "count", "milestone", "dma_sem", "dma_count", "epoch")

    def __init__(self, eng, fn, is_dma=False):
        self.eng = eng
        self.fn = fn
        self.deps = []
        self.is_dma = is_dma
        self.milestone = False
        self.count = None
        self.dma_sem = None
        self.dma_count = None
        self.epoch = 0


class Prog:
    def __init__(self, nc):
        self.nc = nc
        self.ops = {e: [] for e in ("tensor", "vector", "scalar", "gpsimd", "sync")}
        self.state = {}
        self.dma_sems = {}
        self.epoch = 0

    def _entries(self, key):
        name, sub = key if isinstance(key, tuple) else (key, None)
        d = self.state.setdefault(name, {})
        if sub is None:
            if None not in d:
                d[None] = [None, []]
            return list(d.values())
        out = []
        if sub not in d:
            d[sub] = [None, []]
        out.append(d[sub])
        if None in d:
            out.append(d[None])
        return out

    def _add_dep(self, op, dep):
        if dep is None or dep is op:
            return
        if dep.eng == op.eng and not dep.is_dma and not op.is_dma and op.eng == "tensor":
            return
        op.deps.append(dep)

    def add(self, eng, fn, reads=(), writes=(), is_dma=False, dma_key=None):
        op = Op(eng, fn, is_dma)
        op.epoch = self.epoch
        for k in reads:
            for ent in self._entries(k):
                self._add_dep(op, ent[0])
        for k in writes:
            for ent in self._entries(k):
                self._add_dep(op, ent[0])
                for r in ent[1]:
                    self._add_dep(op, r)
        for k in reads:
            name, sub = k if isinstance(k, tuple) else (k, None)
            lst = self.state[name][sub][1]
            if not is_dma:
                lst[:] = [r for r in lst if r.is_dma or r.eng != eng]
            lst.append(op)
        for k in writes:
            name, sub = k if isinstance(k, tuple) else (k, None)
            d = self.state[name]
            if sub is None:
                for s in list(d.keys()):
                    if s is not None:
                        del d[s]
            d[sub] = [op, []]
        frozen = []
        for dep in op.deps:
            if dep.is_dma:
                frozen.append(("dma", dep.dma_sem, self.dma_sems[dep.dma_sem][1]))
            else:
                dep.milestone = True
                frozen.append(("op", dep))
        op.deps = frozen
        if is_dma:
            ds = self.dma_sems.setdefault(dma_key, [None, 0])
            ds[1] += 16
            op.dma_sem = dma_key
            op.dma_count = ds[1]
        self.ops[eng].append(op)
        return op

    def barrier(self):
        last = {}
        for e in self.ops:
            for op in reversed(self.ops[e]):
                if op.epoch != self.epoch or op.fn is None:
                    break
                if not op.is_dma and not getattr(op.fn, "_is_barrier", False):
                    last[e] = op
                    break
        dma_counts = {k: v[1] for k, v in self.dma_sems.items()}
        for e in self.ops:
            def _bnop(eng):
                return eng.nop(nofuse=True)
            _bnop._is_barrier = True
            op = Op(e, _bnop)
            op.epoch = self.epoch
            for e2, l in last.items():
                if e2 != e:
                    l.milestone = True
                    op.deps.append(("op", l))
            for k, c in dma_counts.items():
                if c > 0:
                    op.deps.append(("dma", k, c))
            self.ops[e].append(op)
        self.state = {}
        self.epoch += 1

    def emit(self):
        nc = self.nc
        sems = {}
        for k, ds in self.dma_sems.items():
            ds[0] = nc.alloc_semaphore(f"sd_{k}")
        self.maxcount = {}
        for e in self.ops:
            cnt = {}
            for op in self.ops[e]:
                if op.is_dma:
                    continue
                if op.milestone:
                    c = cnt.get(op.epoch, 0) + 1
                    cnt[op.epoch] = c
                    op.count = c
                    if (e, op.epoch) not in sems:
                        sems[(e, op.epoch)] = nc.alloc_semaphore(f"se_{e}_{op.epoch}")
            self.maxcount[e] = max(list(cnt.values()) + [0])
        assert max(self.maxcount.values()) < 8000, self.maxcount
        with nc.Block() as block:
            def make(ename):
                def body(eng):
                    known = {}
                    for op in self.ops[ename]:
                        need = {}
                        for dep in op.deps:
                            if dep[0] == "dma":
                                sem = self.dma_sems[dep[1]][0]
                                val = dep[2]
                            else:
                                sem = sems[(dep[1].eng, dep[1].epoch)]
                                val = dep[1].count
                            key = id(sem)
                            if val > need.get(key, (None, 0))[1]:
                                need[key] = (sem, val)
                        for key, (sem, val) in need.items():
                            if known.get(key, 0) >= val:
                                continue
                            known[key] = val
                            eng.wait_ge(sem, val)
                        ins = op.fn(eng)
                        if op.is_dma:
                            ins.then_inc(self.dma_sems[op.dma_sem][0], 16)
                        elif op.milestone:
                            ins.then_inc(sems[(ename, op.epoch)], 1)
                return body
            block.tensor(make("tensor"))
            block.vector(make("vector"))
            block.scalar(make("scalar"))
            block.gpsimd(make("gpsimd"))
            block.sync(make("sync"))


def build_program(stage=99, sub=99):
    nc = bass.Bass("TRN2", target_bir_lowering=False)
    p = Prog(nc)

    def din(name, shape):
        return nc.dram_tensor(name, list(shape), F32, kind="ExternalInput").ap()

    x_d = din("x", [TX, D])
    ctx_d = din("ctx", [TCX, D])
    cvec_d = din("cvec", [16, 128])
    adaw_d = din("ada_w", [2, D, 9 * D])
    adab_d = din("ada_b", [2, 72, 128])
    fwi_d = din("ffn_w_in", [2, 2, D, 2 * DFF])
    fwo_d = din("ffn_w_out", [2, 2, DFF, D])
    ewi_d = din("even_w_in", [D, 3088])
    ewo_d = din("even_w_out", [D, D])
    ewh_d = din("even_wh", [D, 2048])
    conv_d = din("conv", [24, 128])
    gateb_d = din("gate_b", [1, 16])
    mnorm_d = din("mnorm", [1, 512])
    sgun_d = din("sgun", [1, 512])
    sgub_d = din("sgub", [1, 512])
    sguws_d = din("sgu_ws", [4, 128, 128])
    owq_d = din("odd_wq", [D, 2048])
    owk_d = din("odd_wk", [D, 1024])
    owv_d = din("odd_wv", [D, 256])
    owo_d = din("odd_w_out", [D, D])
    sink_d = din("sink", [1, 16])
    fnorm_d = din("fnorm", [8, 128])
    ropec_d = din("ropec", [128, TX])
    ropes_d = din("ropes", [128, TX])
    n_out_tok = T if stage < 99 else TX
    out_d = nc.dram_tensor("out", [n_out_tok, D], F32, kind="ExternalOutput").ap()

    def sb(name, shape, dt):
        return nc.alloc_sbuf_tensor(name, list(shape), dt)

    hT = sb("hT", [128, KC, T], F32)
    nT = sb("nT", [128, KC, T], BF16)
    ident_f = sb("ident_f", [128, 128], F32)
    ones_f = sb("ones_f", [128, 128], F32)
    triF = sb("triF", [128, 128], F32)
    triB = sb("triB", [128, 128], F32)
    ident_b = sb("ident_b", [128, 128], BF16)
    ones_b = sb("ones_b", [128, 128], BF16)
    stg = sb("stg", [128, 128], F32)
    cT = sb("cT", [128, 16], F32)
    sT_b = sb("sT_b", [128, 16], BF16)
    adabT = sb("adabT", [128, 2, 72], F32)
    MOD = sb("MOD", [128, 2, 72, 2], F32)
    fnormT = sb("fnormT", [128, 8], F32)
    convT = sb("convT", [128, 24], F32)
    esink = sb("esink", [128, 16], F32)
    wsT = sb("wsT", [128, 4, 128], BF16)
    epsc = sb("epsc", [128, 1], F32)
    arena_base = (nc.sbuf_base + 63) // 64 * 64
    assert arena_base + ARENA <= nc.sbuf_top, (arena_base, nc.sbuf_top)
    acnt = [0, 0]

    def arena_reset():
        acnt[1] = 0

    def at(name, shape, dt):
        acnt[0] += 1
        esz = 4 if dt == F32 else 2
        n = 1
        for s_ in shape[1:]:
            n *= s_
        off = acnt[1]
        acnt[1] = (off + n * esz + 31) // 32 * 32
        assert acnt[1] <= ARENA, (name, off, n * esz)
        return nc.alloc_sbuf_tensor_at(f"{name}_{acnt[0]}", list(shape), dt, offset=arena_base + off)

    psbig = nc.alloc_psum_tensor("psbig", [128, 8, 512], F32)
    ps = [psbig[:, i, :] for i in range(8)]
    psb = [ps[i].bitcast(BF16) for i in range(8)]

    def MM(out, lhsT, rhs, start, stop, rd, wr):
        p.add("tensor", lambda e: e.matmul(out, lhsT, rhs, start=start, stop=stop), reads=rd, writes=wr)

    def TR(out, in_, ident, rd, wr):
        p.add("tensor", lambda e: e.transpose(out, in_, ident), reads=rd, writes=wr)

    def ACT(out, in_, func, rd, wr, bias=None, scale=None):
        kw = {}
        if bias is not None:
            kw["bias"] = bias
        if scale is not None:
            kw["scale"] = scale
        p.add("scalar", lambda e: e.activation(out=out, in_=in_, func=func, **kw), reads=rd, writes=wr)

    def TT(eng, out, in0, in1, op, rd, wr):
        p.add(eng, lambda e: e.tensor_tensor(out=out, in0=in0, in1=in1, op=op), reads=rd, writes=wr)

    def TS(eng, out, in0, s1, s2, op0, op1, rd, wr):
        if s2 is None:
            p.add(eng, lambda e: e.tensor_scalar(out=out, in0=in0, scalar1=s1, scalar2=None, op0=op0), reads=rd, writes=wr)
        else:
            p.add(eng, lambda e: e.tensor_scalar(out=out, in0=in0, scalar1=s1, scalar2=s2, op0=op0, op1=op1), reads=rd, writes=wr)

    def STT(eng, out, in0, scalar, in1, op0, op1, rd, wr):
        p.add(eng, lambda e: e.scalar_tensor_tensor(out=out, in0=in0, scalar=scalar, in1=in1, op0=op0, op1=op1), reads=rd, writes=wr)

    def CP(eng, out, in_, rd, wr):
        if eng == "scalar":
            p.add(eng, lambda e: e.copy(out=out, in_=in_), reads=rd, writes=wr)
        else:
            p.add(eng, lambda e: e.tensor_copy(out=out, in_=in_), reads=rd, writes=wr)

    def MSET(eng, ap, val, wr):
        p.add(eng, lambda e: e.memset(ap, val), writes=wr)

    def RECIP(out, in_, rd, wr):
        p.add("vector", lambda e: e.reciprocal(out=out, in_=in_), reads=rd, writes=wr)

    def RSUM(out, in_, rd, wr):
        p.add("vector", lambda e: e.reduce_sum(out=out, in_=in_, axis=AX.X), reads=rd, writes=wr)

    def DMA(q, out, in_, rd, wr, key):
        p.add(q, lambda e: e.dma_start(out=out, in_=in_), reads=rd, writes=wr, is_dma=True, dma_key=key)

    def ASEL(out, in_, pattern, cm, cmp, fill, rd, wr):
        p.add("gpsimd", lambda e: e.affine_select(out=out, in_=in_, pattern=pattern, compare_op=cmp, fill=fill, base=0, channel_multiplier=cm), reads=rd, writes=wr)

    def st_of(tile):
        return 0 if tile < 2 else 256 + ((tile - 2) // 4) * 512

    MSET("gpsimd", ones_f[:], 1.0, ["ones_f"])
    MSET("gpsimd", ones_b[:], 1.0, ["ones_b"])
    MSET("gpsimd", ident_f[:], 0.0, ["ident_f"])
    MSET("gpsimd", epsc[:], EPS, ["epsc"])
    ASEL(ident_f[:], ident_f[:], [[-1, 128]], 1, ALU.not_equal, 1.0, ["ident_f"], ["ident_f"])
    ASEL(triF[:], ones_f[:], [[1, 128]], -1, ALU.is_ge, 0.0, ["ones_f"], ["triF"])
    ASEL(triB[:], ones_f[:], [[-1, 128]], 1, ALU.is_ge, 0.0, ["ones_f"], ["triB"])
    CP("gpsimd", ident_b[:], ident_f[:], ["ident_f"], ["ident_b"])

    def load_cols(dram_ap, n, dest_ap, destkey):
        DMA("sync", stg[0:n, :], dram_ap, [], ["stg"], "stg")
        TR(ps[7][:, 0:n], stg[0:n, :], ident_f[0:n, 0:n], ["stg", "ident_f"], [("ps", 7)])
        CP("vector", dest_ap, ps[7][:, 0:n], [("ps", 7)], [destkey])

    load_cols(cvec_d, 16, cT[:], "cT")
    ACT(sT_b[:], cT[:], AF.Silu, ["cT"], ["sT_b"])
    for l in range(2):
        load_cols(adab_d[l], 72, adabT[:, l, :], ("adabT", l))
    load_cols(fnorm_d, 8, fnormT[:], "fnormT")
    load_cols(conv_d, 24, convT[:], "convT")
    DMA("sync", esink[:], sink_d[0:1, :].to_broadcast([128, 16]), [], ["esink"], "esink")
    ACT(esink[:], esink[:], AF.Exp, ["esink"], ["esink"])
    arena_reset()
    ws_stage = at("ws_stage", [128, 4, 128], F32)
    DMA("sync", ws_stage[:], sguws_d.rearrange("g p q -> p g q"), [], ["ws_stage"], "wsst")
    for g in range(4):
        TR(ps[6][:, g * 128:(g + 1) * 128], ws_stage[:, g, :], ident_f[:], ["ws_stage", "ident_f"], [("ps", 6)])
    CP("vector", wsT[:].rearrange("p g q -> p (g q)"), ps[6][:], [("ps", 6)], ["wsT"])
    p.barrier()

    def mcol(l, i, kc, j):
        return MOD[:, l, i * 8 + kc, j:j + 1]

    def modulation_gen(l, A, blocks, chunks):
        wv = adaw_d[l].rearrange("(kc p) n -> p kc n", p=128)
        for bi, blk in enumerate(blocks):
            s = bi % 2
            DMA("gpsimd", A[s][:], wv[:, :, blk * 512:(blk + 1) * 512], [], [("adaA", s)], f"adaA{s}")
            if bi >= 1:
                yield
            for n in range(4):
                col = (blk * 4 + n) * 2
                for kc in range(KC):
                    MM(ps[6][:, col:col + 2], A[s][:, kc, n * 128:(n + 1) * 128], sT_b[:, kc:16:8], kc == 0, kc == KC - 1,
                       [("adaA", s), "sT_b"], [("ps", 6)])
        c0, c1 = chunks
        mk = ("MOD", (l, c0))
        for j in range(2):
            TT("vector", MOD[:, l, c0:c1, j], ps[6][:, 2 * c0 + j:2 * c1:2], adabT[:, l, c0:c1], ALU.add, [("ps", 6), ("adabT", l)], [mk])
        for i in (1, 4, 7):
            if c0 <= i * 8 < c1:
                TS("vector", MOD[:, l, i * 8:(i + 1) * 8, :], MOD[:, l, i * 8:(i + 1) * 8, :], 1.0, None, ALU.add, None, [mk], [mk])
        for i in (2, 8):
            if c0 <= i * 8 < c1:
                TS("vector", MOD[:, l, i * 8:(i + 1) * 8, :], MOD[:, l, i * 8:(i + 1) * 8, :], 0.5, None, ALU.mult, None, [mk], [mk])
        yield

    def modulation_head(l):
        A = [at("adaA", [128, KC, 512], BF16) for s in range(2)]
        for _ in modulation_gen(l, A, list(range(0, 6)), (0, 24)):
            pass

    def load_tokens():
        xs = [at("xstage", [128, D], F32) for s in range(2)]
        for tile in range(NTILE):
            s = tile % 2
            src = ctx_d[tile * 128:(tile + 1) * 128, :] if tile < 2 else x_d[(tile - 2) * 128:(tile - 1) * 128, :]
            DMA("sync", xs[s][:], src, [], [("xs", s)], f"xs{s}")
            for half in range(2):
                b = 2 * s + half
                for q in range(4):
                    kc = half * 4 + q
                    TR(ps[b][:, q * 128:(q + 1) * 128], xs[s][:, kc * 128:(kc + 1) * 128], ident_f[:], [("xs", s), "ident_f"], [("ps", b)])
                eng = "vector" if half == 0 else "scalar"
                CP(eng, hT[:, half * 4:(half + 1) * 4, tile * 128:(tile + 1) * 128], ps[b][:].rearrange("p (q t) -> p q t", q=4),
                   [("ps", b)], [("hT", tile)])

    def norm_phase(l, i_shift, ranges, final=False, fin_out=None):
        sqf = at("sq", [128, KC, 512], BF16) if final else None
        rstd = [at("rstd", [128, 512], F32) for s in range(2)]
        tmp = [at("ntmp", [128, 512], F32) for s in range(4)]
        ranges = list(ranges)

        def sq_ap(t0, t1):
            return sqf[:, :, 0:t1 - t0] if final else nT[:, :, t0:t1]

        def sq_key(t0):
            return "sq" if final else ("nT", t0)

        def square(i):
            t0, t1 = ranges[i]
            tk = [("hT", t) for t in range(t0 // 128, t1 // 128)]
            ACT(sq_ap(t0, t1), hT[:, :, t0:t1], AF.Square, tk, [sq_key(t0)])

        square(0)
        for i, (t0, t1) in enumerate(ranges):
            n = t1 - t0
            j = 1 if t0 == 0 else 0
            r = rstd[i % 2]
            rk = ("rstd", i % 2)
            tk = [("hT", t) for t in range(t0 // 128, t1 // 128)]
            sqa = sq_ap(t0, t1)
            for kc in range(KC):
                MM(ps[7][:, 0:n], ones_b[:], sqa[:, kc, :], kc == 0, kc == KC - 1, [sq_key(t0), "ones_b"], [("ps", 7)])
            ACT(r[:, 0:n], ps[7][:, 0:n], AF.Sqrt, [("ps", 7), "epsc"], [rk], bias=epsc[:], scale=1.0 / D)
            RECIP(r[:, 0:n], r[:, 0:n], [rk], [rk])
            if i + 1 < len(ranges) and not final:
                square(i + 1)
            for kc in range(KC):
                s = kc % 4
                if final:
                    STT("vector", fin_out[:, kc, 0:n], hT[:, kc, t0:t1], fnormT[:, kc:kc + 1], r[:, 0:n], ALU.mult, ALU.mult,
                        tk + [rk, "fnormT"], [("fin", kc)])
                else:
                    STT("vector", tmp[s][:, 0:n], hT[:, kc, t0:t1], mcol(l, i_shift + 1, kc, j), r[:, 0:n], ALU.mult, ALU.mult,
                        tk + [rk, "MOD"], [("ntmp", s)])
                    if kc % 4 != 3:
                        ACT(nT[:, kc, t0:t1], tmp[s][:, 0:n], AF.Identity, [("ntmp", s), "MOD"], [("nT", t0)],
                            bias=mcol(l, i_shift, kc, j))
                    else:
                        TS("vector", nT[:, kc, t0:t1], tmp[s][:, 0:n], mcol(l, i_shift, kc, j), None, ALU.add, None,
                           [("ntmp", s), "MOD"], [("nT", t0)])
            if i + 1 < len(ranges) and final:
                square(i + 1)

    def resid(psb_, n, dc, t0, t1, l, i_gate):
        j = 1 if t0 == 0 else 0
        tk = [("hT", t) for t in range(t0 // 128, t1 // 128)]
        STT("vector", hT[:, dc, t0:t1], ps[psb_][:, 0:n], mcol(l, i_gate, dc, j), hT[:, dc, t0:t1], ALU.mult, ALU.add,
            [("ps", psb_), "MOD"] + tk, tk)

    def ffn(l, jf, i_base, ranges, side=None, end_barrier=True):
        arena_reset()
        WIg = [at("WIg", [128, KC, 512], BF16) for s in range(2)]
        WIu = [at("WIu", [128, KC, 512], BF16) for s in range(2)]
        WO = [at("WO", [128, 4, D], BF16) for s in range(2)]
        hid = [at("hid", [128, 4, 512], BF16) for s in range(2)]
        sg = [at("sg", [128, 512], BF16) for s in range(3)]
        win = fwi_d[l, jf].rearrange("(kc p) n -> p kc n", p=128)
        wout = fwo_d[l, jf].rearrange("(fc p) n -> p fc n", p=128)
        groups = [(0, 4), (4, 4), (8, 4), (12, 4), (16, 3), (19, 3)]

        def load_group(gi):
            f0, nf = groups[gi]
            s = gi % 2
            DMA("gpsimd", WIg[s][:, :, 0:nf * 128], win[:, :, f0 * 128:(f0 + nf) * 128], [], [("WIg", s)], f"wig{s}")
            DMA("gpsimd", WIu[s][:, :, 0:nf * 128], win[:, :, DFF + f0 * 128:DFF + (f0 + nf) * 128], [], [("WIu", s)], f"wiu{s}")
            DMA("gpsimd", WO[s][:, 0:nf, :], wout[:, f0:f0 + nf, :], [], [("WO", s)], f"wo{s}")

        fcnt = [0]

        def GU(gi, t0, t1, hs):
            f0, nf = groups[gi]
            s = gi % 2
            n = t1 - t0
            for fi in range(nf):
                k5 = fcnt[0]
                fcnt[0] += 1
                bg_, bu_ = (2 * k5) % 5, (2 * k5 + 1) % 5
                b = k5 % 3
                for kc in range(KC):
                    MM(ps[bg_][:, 0:n], WIg[s][:, kc, fi * 128:(fi + 1) * 128], nT[:, kc, t0:t1], kc == 0, kc == KC - 1,
                       [("WIg", s), ("nT", t0)], [("ps", bg_)])
                for kc in range(KC):
                    MM(ps[bu_][:, 0:n], WIu[s][:, kc, fi * 128:(fi + 1) * 128], nT[:, kc, t0:t1], kc == 0, kc == KC - 1,
                       [("WIu", s), ("nT", t0)], [("ps", bu_)])
                ACT(sg[b][:, 0:n], ps[bg_][:, 0:n], AF.Silu, [("ps", bg_)], [("sg", b)])
                TT("vector", hid[hs][:, fi, 0:n], sg[b][:, 0:n], ps[bu_][:, 0:n], ALU.mult, [("sg", b), ("ps", bu_)], [("hid", (hs, fi))])

        def WOUT(gi, t0, t1, hs):
            f0, nf = groups[gi]
            s = gi % 2
            n = t1 - t0
            for dc in range(KC):
                b = 5 if dc % 2 == 0 else 7
                for fi in range(nf):
                    MM(ps[b][:, 0:n], WO[s][:, fi, dc * 128:(dc + 1) * 128], hid[hs][:, fi, 0:n], fi == 0, fi == nf - 1,
                       [("WO", s), ("hid", (hs, fi))], [("ps", b)])
                resid(b, n, dc, t0, t1, l, i_base + 2)

        load_group(0)
        load_group(1)
        norm_phase(l, i_base, ranges)
        items = [(gi, t0, t1) for gi in range(len(groups)) for (t0, t1) in ranges]
        side_gen = None
        if side is not None:
            A = [at("adaA", [128, KC, 512], BF16) for s in range(2)]
            side_gen = modulation_gen(side[0], A, side[1], side[2])
        prev = None
        for k, (gi, t0, t1) in enumerate(items):
            GU(gi, t0, t1, k % 2)
            if prev is not None:
                WOUT(*prev)
                if prev[0] != gi and gi + 1 < len(groups):
                    load_group(gi + 1)
            prev = (gi, t0, t1, k % 2)
            if side_gen is not None and k >= 1:
                try:
                    next(side_gen)
                except StopIteration:
                    side_gen = None
        WOUT(*prev)
        if side_gen is not None:
            for _ in side_gen:
                pass
        if end_barrier:
            p.barrier()

    def even_mixer(l):
        arena_reset()
        norm_phase(l, 3, ST_ALL)
        Wu = at("Wu", [128, KC, 512], BF16)
        Wv = at("Wv", [128, KC, 512], BF16)
        HBT = at("HBT", [128, 4, T], BF16)
        sgun_bc = at("sgun_bc", [128, 512], F32)
        sgub_bc = at("sgub_bc", [128, 512], F32)
        Wo4 = at("Wo4", [128, 4, D], BF16)
        vt = [at("vt", [128, 512], F32) for s in range(2)]
        ut = [at("ut", [128, 512], F32) for s in range(2)]
        vn = [at("vn", [128, 512], BF16) for s in range(2)]
        sqj = at("sqj", [128, 512], F32)
        mt = [at("mt", [128, 512], F32) for s in range(2)]
        scol = [at("scol", [128, 4], F32) for s in range(2)]
        wi = ewi_d.rearrange("(kc p) n -> p kc n", p=128)
        DMA("gpsimd", Wu[:], wi[:, :, 2064:2576], [], ["Wu"], "Wu")
        DMA("gpsimd", Wv[:], wi[:, :, 2576:3088], [], ["Wv"], "Wv_e")
        DMA("gpsimd", Wo4[:], ewo_d[512:1024, :].rearrange("(g p) n -> p g n", p=128), [], ["Wo4"], "Wo4")
        DMA("sync", sgun_bc[:], sgun_d[0:1, :].to_broadcast([128, 512]), [], ["sgun_bc"], "sgun")
        DMA("sync", sgub_bc[:], sgub_d[0:1, :].to_broadcast([128, 512]), [], ["sgub_bc"], "sgub")
        def sgu_proj(tile):
            a, b_ = tile * 128, (tile + 1) * 128
            s = tile % 2
            b0, b1 = 3 * s, 3 * s + 1
            nk = ("nT", st_of(tile))
            for kc in range(KC):
                MM(ps[b0][:], nT[:, kc, a:b_], Wv[:, kc, :], kc == 0, kc == KC - 1, [nk, "Wv"], [("ps", b0)])
            ACT(vt[s][:], ps[b0][:], AF.Gelu_apprx_tanh, [("ps", b0)], [("vt", s)])
            ACT(sqj[:], vt[s][:], AF.Square, [("vt", s)], ["sqj"])
            RSUM(scol[s][:, 0:1], sqj[:], ["sqj"], [("scol", s)])
            ACT(scol[s][:, 1:2], scol[s][:, 0:1], AF.Sqrt, [("scol", s), "epsc"], [("scol", s)], bias=epsc[:], scale=1.0 / 512)
            RECIP(scol[s][:, 2:3], scol[s][:, 1:2], [("scol", s)], [("scol", s)])
            STT("vector", vn[s][:], vt[s][:], scol[s][:, 2:3], sgun_bc[:], ALU.mult, ALU.mult, [("vt", s), ("scol", s), "sgun_bc"], [("vn", s)])
            for g in range(4):
                for kc in range(KC):
                    MM(ps[b1][:, g * 128:(g + 1) * 128], Wu[:, kc, g * 128:(g + 1) * 128], nT[:, kc, a:b_], kc == 0, kc == KC - 1,
                       [nk, "Wu"], [("ps", b1)])
            ACT(ut[s][:], ps[b1][:], AF.Gelu_apprx_tanh, [("ps", b1)], [("ut", s)])

        def sgu_mix(tile):
            a, b_ = tile * 128, (tile + 1) * 128
            s = tile % 2
            b2 = 3 * s + 2
            for g in range(4):
                MM(ps[b2][:, g * 128:(g + 1) * 128], vn[s][:, g * 128:(g + 1) * 128], wsT[:, g, :], True, True, [("vn", s), "wsT"], [("ps", b2)])
            TT("vector", mt[s][:], ps[b2][:], sgub_bc[:], ALU.add, [("ps", b2), "sgub_bc"], [("mt", s)])
            TT("gpsimd", HBT[:, :, a:b_], mt[s][:].rearrange("p (g t) -> p g t", g=4), ut[s][:].rearrange("p (g t) -> p g t", g=4), ALU.mult,
               [("mt", s), ("ut", s)], [("HBT", tile)])

        sgu_proj(0)
        for tile in range(NTILE):
            if tile + 1 < NTILE:
                sgu_proj(tile + 1)
            sgu_mix(tile)
        for (t0, t1) in ST_ALL:
            n = t1 - t0
            tks = [("HBT", t) for t in range(t0 // 128, t1 // 128)]
            for dc in range(KC):
                b = 6 + dc % 2
                for g in range(4):
                    MM(ps[b][:, 0:n], Wo4[:, g, dc * 128:(dc + 1) * 128], HBT[:, g, t0:t1], g == 0, g == 3, ["Wo4"] + tks, [("ps", b)])
                resid(b, n, dc, t0, t1, l, 5)
        p.barrier()
        arena_reset()
        Wh = [at("Wh", [128, KC, 512], BF16)] * 2
        Woh = [at("Woh", [128, D], BF16) for s in range(2)]
        pre = at("pre", [128, T], F32)
        qT = at("qT", [128, T], BF16)
        kT = at("kT", [128, T], BF16)
        kTok = at("kTok", [128, NTILE, 128], BF16)
        vaug = at("vaug", [128, NTILE, 130], BF16)
        og = at("og", [128, NTILE, 128], BF16)
        HF = at("HF", [128, NTILE, 128], BF16)
        Cb = [at("Cb", [128, NTILE, 130], BF16) for d in range(2)]
        HAT = at("HAT", [128, T], BF16)
        mnorm_bc = at("mnorm_bc", [128, 512], F32)
        EA = [at("EA", [128, NTILE, 4], F32) for d in range(2)]
        EB = [at("EB", [128, NTILE, 4], F32) for d in range(2)]
        EG = [at("EG", [128, NTILE, 4], F32) for d in range(2)]
        C32 = [at("C32", [128, 130], F32) for d in range(2)]
        C32b = [at("C32b", [128, 130], F32) for d in range(2)]
        cacc = [at("cacc", [128, 512], F32) for s in range(2)]
        ktl = [at("ktl", [128, 128], BF16) for s in range(4)]
        PT = [at("PT", [128, 128], BF16) for s in range(3)]
        NDs = at("NDs", [128, NTILE, 130], F32)
        rc = at("rc", [128, 8, NTILE], F32)
        sc = [at("sc", [128, 8], F32) for s in range(2)]
        og2 = [og, at("og2", [128, NTILE, 128], BF16)]
        gate_mark = acnt[1]
        Wg = at("Wg", [128, KC, 16], BF16)
        G = at("G", [128, NTILE, 16], F32)
        LF = at("LF", [128, NTILE, 16], F32)
        gb_bc = at("gb_bc", [128, 16], F32)
        DMA("gpsimd", Wg[:], wi[:, :, 2048:2064], [], ["Wg"], "Wg")
        DMA("sync", mnorm_bc[:], mnorm_d[0:1, :].to_broadcast([128, 512]), [], ["mnorm_bc"], "mnorm")
        DMA("sync", gb_bc[:], gateb_d[0:1, :].to_broadcast([128, 16]), [], ["gb_bc"], "gb")
        MSET("gpsimd", vaug[:], 1.0, ["vaug"])
        for tile in range(NTILE):
            a, b_ = tile * 128, (tile + 1) * 128
            for kc in range(KC):
                MM(ps[0][:, tile * 16:(tile + 1) * 16], nT[:, kc, a:b_], Wg[:, kc, :], kc == 0, kc == KC - 1, [("nT", st_of(tile)), "Wg"], [("ps", 0)])
        v3 = lambda ap: ap.rearrange("p (t g) -> p t g", g=16)
        TT("vector", G[:], v3(ps[0][:, 0:288]), gb_bc[:].unsqueeze(1).to_broadcast([128, NTILE, 16]), ALU.add, [("ps", 0), "gb_bc"], ["G"])
        ACT(LF[:], G[:], AF.Exp, ["G"], ["LF"], scale=-1.0)
        ACT(LF[:], LF[:], AF.Ln, ["LF"], ["LF"], bias=1.0)
        TS("vector", LF[:], LF[:], -1.0, None, ALU.mult, None, ["LF"], ["LF"])
        LF2 = LF[:].rearrange("p t g -> p (t g)")
        MM(ps[1][:, 0:288], triF[:], LF2, True, True, ["triF", "LF"], [("ps", 1)])
        MM(ps[2][:, 0:288], triB[:], LF2, True, True, ["triB", "LF"], [("ps", 2)])
        MM(ps[3][:, 0:288], ones_f[:], LF2, True, True, ["ones_f", "LF"], [("ps", 3)])
        LNS = float(np.log(128.0 ** -0.5))
        lnsc = sc[0][:, 7:8]
        MSET("vector", lnsc, LNS, [("sc", 0)])
        for d in range(2):
            igv = G[:, :, 0:4] if d == 0 else G[:, :, 8:12]
            bv = v3(ps[1][:, 0:288])[:, :, 4:8] if d == 0 else v3(ps[2][:, 0:288])[:, :, 12:16]
            gv = v3(ps[3][:, 0:288])[:, :, 4:8] if d == 0 else v3(ps[3][:, 0:288])[:, :, 12:16]
            pk = ("ps", 1 if d == 0 else 2)
            TT("vector", EA[d][:], igv, bv, ALU.subtract, ["G", pk], [("EA", d)])
            ACT(EA[d][:], EA[d][:], AF.Exp, [("EA", d), ("sc", 0)], [("EA", d)], bias=lnsc)
            ACT(EB[d][:], bv, AF.Exp, [pk], [("EB", d)])
            ACT(EG[d][:], gv, AF.Exp, [("ps", 3)], [("EG", d)])
        p.barrier()
        acnt[1] = gate_mark
        HAT2 = [HAT, at("HATb", [128, T], BF16)]
        pcnt = [0]
        pre3 = pre[:, 0:T].rearrange("p (c e) -> p c e", e=128)
        HAT3 = HAT[:, 0:T].rearrange("p (c e) -> p c e", e=128)
        orders = [list(range(NTILE)), [1, 0] + list(range(NTILE - 1, 1, -1))]

        def run(*gens):
            gens = [g for g in gens if g is not None]
            while gens:
                for g_ in list(gens):
                    try:
                        next(g_)
                    except StopIteration:
                        gens.remove(g_)

        def proj_fm(wcols, dest, cidx):
            for (t0, t1) in ST_ALL:
                n = t1 - t0
                b = pcnt[0] % 2
                pcnt[0] += 1
                for kc in range(KC):
                    MM(ps[b][:, 0:n], wcols[:, kc, :], nT[:, kc, t0:t1], kc == 0, kc == KC - 1, [("nT", t0), ("Wh", 0)], [("ps", b)])
                CP("scalar", pre[:, t0:t1], ps[b][:, 0:n], [("ps", b)], [("pre", t0)])
                yield
            for si, (t0, t1) in enumerate(ST_ALL):
                n = t1 - t0
                s0, s1 = (0, 256) if t0 == 0 else (256, T)
                ca = cacc[pcnt[0] % 2]
                ck = ("cacc", pcnt[0] % 2)
                pcnt[0] += 1
                pk = [("pre", ST_ALL[i][0]) for i in (si - 1, si, si + 1) if 0 <= i < len(ST_ALL)]
                w = lambda tap: convT[:, tap * 8 + cidx:tap * 8 + cidx + 1]
                ACT(ca[:, 0:n], pre[:, t0:t1], AF.Identity, pk + ["convT"], [ck], scale=w(1))
                lo = max(t0, s0 + 1)
                STT("vector", ca[:, lo - t0:n], pre[:, lo - 1:t1 - 1], w(0), ca[:, lo - t0:n], ALU.mult, ALU.add, pk + ["convT", ck], [ck])
                hi = min(t1, s1 - 1)
                STT("vector", ca[:, 0:hi - t0], pre[:, t0 + 1:hi + 1], w(2), ca[:, 0:hi - t0], ALU.mult, ALU.add, pk + ["convT", ck], [ck])
                ACT(dest[:, t0:t1], ca[:, 0:n], AF.Silu, [ck], [dest.name])
                yield

        def P(h):
            s = h % 2
            ogh = og2[s]
            yield from proj_fm(Wh[s][:, :, 0:128], qT, h)
            for t4 in range(0, NTILE, 4):
                nt = min(4, NTILE - t4)
                for q in range(nt):
                    tile = t4 + q
                    for kc in range(KC):
                        MM(ps[3][:, q * 128:(q + 1) * 128], nT[:, kc, tile * 128:(tile + 1) * 128], Wh[s][:, kc, 256:384], kc == 0, kc == KC - 1,
                           [("nT", st_of(tile)), ("Wh", 0)], [("ps", 3)])
                CP("scalar", vaug[:, t4:t4 + nt, 0:128], ps[3][:, 0:nt * 128].rearrange("p (q t) -> p q t", q=nt), [("ps", 3)], ["vaug"])
                yield
                for q in range(nt):
                    tile = t4 + q
                    for kc in range(KC):
                        MM(ps[4][:, q * 128:(q + 1) * 128], nT[:, kc, tile * 128:(tile + 1) * 128], Wh[s][:, kc, 384:512], kc == 0, kc == KC - 1,
                           [("nT", st_of(tile)), ("Wh", 0)], [("ps", 4)])
                ACT(ogh[:, t4:t4 + nt, :], ps[4][:, 0:nt * 128].rearrange("p (q t) -> p q t", q=nt), AF.Sigmoid, [("ps", 4)], [("og", s)])
                yield
            yield from proj_fm(Wh[s][:, :, 128:256], kT, 4 + h)
            for t4 in range(0, NTILE, 4):
                nt = min(4, NTILE - t4)
                for q in range(nt):
                    tile = t4 + q
                    TR(psb[2][:, q * 128:(q + 1) * 128], kT[:, tile * 128:(tile + 1) * 128], ident_b[:], [kT.name, "ident_b"], [("ps", 2)])
                CP("vector", kTok[:, t4:t4 + nt, :], psb[2][:, 0:nt * 128].rearrange("p (q t) -> p q t", q=nt), [("ps", 2)], ["kTok"])
                yield

        scnt = [0]

        def S(h, d):
            order = orders[d]
            Ca = [C32[d], C32b[d]]
            kt2 = ktl[2 * d:2 * d + 2]
            MSET("vector", Ca[0][:], 0.0, [("C32", (d, 0))])
            MSET("vector", Cb[d][:, order[0], :], 0.0, [("Cb", (d, order[0]))])

            def mk_ktl(idx):
                c = order[idx]
                i2 = idx % 2
                ACT(kt2[i2][:], kTok[:, c, :], AF.Identity, ["kTok", ("EA", d)], [("ktl", (d, i2))], scale=EA[d][:, c, h:h + 1])

            mk_ktl(0)
            for idx, c in enumerate(order[:-1]):
                i2 = idx % 2
                pb = 5 + i2
                co, cn = Ca[idx % 2], Ca[(idx + 1) % 2]
                ko, kn = ("C32", (d, idx % 2)), ("C32", (d, (idx + 1) % 2))
                cprev = order[idx - 1] if idx >= 1 else c
                MM(ps[pb][:, d * 256:d * 256 + 129], kt2[i2][:], vaug[:, c, 0:129], True, True, [("ktl", (d, i2)), "vaug"], [("ps", pb)])
                STT("vector", cn[:, 0:129], co[:, 0:129], EG[d][:, cprev, h:h + 1], ps[pb][:, d * 256:d * 256 + 129], ALU.mult, ALU.add,
                    [ko, ("EG", d), ("ps", pb)], [kn])
                if idx + 1 < len(order) - 1:
                    mk_ktl(idx + 1)
                ACT(Cb[d][:, order[idx + 1], 0:129], cn[:, 0:129], AF.Identity, [kn, ("EG", d)], [("Cb", (d, order[idx + 1]))],
                    scale=EG[d][:, c, h:h + 1])
                yield

        def O(h, d):
            mask = triF if d == 0 else triB
            mk = "triF" if d == 0 else "triB"
            order = orders[d]

            SB = (0, 1, 4)
            NB = (2, 3, 7)

            def st_mm(c, i3):
                a, b_ = c * 128, (c + 1) * 128
                sb_ = SB[i3]
                MM(ps[sb_][:, 0:128], kT[:, a:b_], qT[:, a:b_], True, True, [kT.name, qT.name], [("ps", sb_)])
                STT("vector", PT[i3][:], ps[sb_][:, 0:128], EA[d][:, c, h:h + 1], mask[:], ALU.mult, ALU.mult, [("ps", sb_), ("EA", d), mk], [("PT", i3)])

            def nd_mm(c, i3):
                a, b_ = c * 128, (c + 1) * 128
                nb_ = NB[i3]
                MM(ps[nb_][:, 0:129], PT[i3][:], vaug[:, c, 0:129], True, False, [("PT", i3), "vaug"], [("ps", nb_)])
                MM(ps[nb_][:, 0:129], qT[:, a:b_], Cb[d][:, c, 0:129], False, True, [qT.name, ("Cb", (d, c))], [("ps", nb_)])
                CP("scalar", NDs[:, c, 0:129], ps[nb_][:, 0:129], [("ps", nb_)], [("NDs", c)])

            st_mm(order[0], 0)
            if NTILE > 1:
                st_mm(order[1], 1)
            for k, c in enumerate(order):
                if k + 2 < NTILE:
                    st_mm(order[k + 2], (k + 2) % 3)
                nd_mm(c, k % 3)
                yield
            ebv = EB[d][:, :, h]
            TT("vector", rc[:, 0, :], NDs[:, :, 128], ebv, ALU.mult, ["NDs", ("EB", d)], ["rc"])
            ACT(rc[:, 1, :], rc[:, 0, :], AF.Abs, ["rc"], ["rc"])
            TS("vector", rc[:, 1, :], rc[:, 1, :], 1.0, None, ALU.max, None, ["rc"], ["rc"])
            RECIP(rc[:, 2, :], rc[:, 1, :], ["rc"], ["rc"])
            TT("vector", rc[:, 3, :], rc[:, 2, :], ebv, ALU.mult, ["rc", ("EB", d)], ["rc"])
            yield
            r_bc = rc[:, 3, :].unsqueeze(2).to_broadcast([128, NTILE, 128])
            ndv = NDs[:, :, 0:128]
            if d == 0:
                TT("vector", HF[:], ndv, r_bc, ALU.mult, ["NDs", "rc"], ["HF"])
            else:
                TT("vector", ndv, ndv, r_bc, ALU.mult, ["NDs", "rc"], ["NDs"])
            yield

        def NW(h):
            s = h % 2
            HATs = HAT2[s]
            HAT3 = HATs[:, 0:T].rearrange("p (c e) -> p c e", e=128)
            hk = ("HAT", s)
            ndv = NDs[:, :, 0:128]
            TT("vector", ndv, ndv, HF[:], ALU.add, ["NDs", "HF"], ["NDs"])
            yield
            ACT(HAT3, ndv, AF.Square, ["NDs"], [hk])
            yield
            RSUM(rc[:, 4, :], HAT3, [hk], ["rc"])
            ACT(rc[:, 5, :], rc[:, 4, :], AF.Sqrt, ["rc", "epsc"], ["rc"], bias=epsc[:], scale=1.0 / 128)
            RECIP(rc[:, 6, :], rc[:, 5, :], ["rc"], ["rc"])
            yield
            TT("vector", ndv, ndv, rc[:, 6, :].unsqueeze(2).to_broadcast([128, NTILE, 128]), ALU.mult, ["NDs", "rc"], ["NDs"])
            yield
            TT("vector", ndv, ndv, mnorm_bc[:, h * 128:(h + 1) * 128].unsqueeze(1).to_broadcast([128, NTILE, 128]), ALU.mult,
               ["NDs", "mnorm_bc"], ["NDs"])
            yield
            TT("vector", HF[:], ndv, og2[s][:], ALU.mult, ["NDs", ("og", s)], ["HF"])
            yield
            for t4 in range(0, NTILE, 4):
                nt = min(4, NTILE - t4)
                for q in range(nt):
                    TR(psb[5][:, q * 128:(q + 1) * 128], HF[:, t4 + q, :], ident_b[:], ["HF", "ident_b"], [("ps", 5)])
                CP("scalar", HATs[:, t4 * 128:(t4 + nt) * 128], psb[5][:, 0:nt * 128], [("ps", 5)], [hk])
                yield
            if h % 2 == 1:
                for (t0, t1) in ST_ALL:
                    n = t1 - t0
                    for dc in range(KC):
                        b = 6 + dc % 2
                        for h2 in range(2):
                            MM(ps[b][:, 0:n], Woh[h2][:, dc * 128:(dc + 1) * 128], HAT2[h2][:, t0:t1], h2 == 0, h2 == 1,
                               [("Woh", h2), ("HAT", h2)], [("ps", b)])
                        resid(b, n, dc, t0, t1, l, 5)
                        if dc % 4 == 3:
                            yield

        wih = ewh_d.rearrange("(kc p) n -> p kc n", p=128)

        def load_head(h):
            s = h % 2
            DMA("gpsimd", Wh[s][:], wih[:, :, h * 512:(h + 1) * 512], [], [("Wh", 0)], "Wh0")

        def load_woh(hp):
            for h2 in range(2):
                DMA("gpsimd", Woh[h2][:], ewo_d[(2 * hp + h2) * 128:(2 * hp + h2 + 1) * 128, :], [], [("Woh", h2)], f"Woh{h2}")

        load_head(0)
        load_woh(0)
        run(P(0))
        load_head(1)
        for h in range(4):
            run(S(h, 0), S(h, 1), O(h, 0))
            run(O(h, 1))
            run(NW(h), P(h + 1) if h + 1 < 4 else None)
            if h + 2 < 4:
                load_head(h + 2)
            if h == 1:
                load_woh(1)
        p.barrier()

    def odd_mixer(l):
        arena_reset()
        norm_phase(l, 3, ST_ALL)
        p.barrier()
        arena_reset()
        KT = at("KT", [128, 4, T], BF16)
        V = at("V", [128, NTILE, 4, 66], BF16)
        ropec = at("ropec", [128, TX], F32)
        ropes = at("ropes", [128, TX], F32)
        Wq2 = [at("Wq2", [128, KC, 256], BF16) for s in range(2)]
        Wv = at("Wv", [128, KC, 256], BF16)
        QT = [at("QT", [128, TX], BF16) for s in range(2)]
        r1 = [at("r1", [128, 512], F32) for s in range(2)]
        r2 = [at("r2", [128, 512], F32) for s in range(2)]
        PT = [at("PTa", [128, 256], BF16) for s in range(10)]
        Ab = [at("Ab", [128, 128], BF16) for s in range(2)]
        ATc = [at("ATc", [128, TX], BF16) for s in range(2)]
        Woc = [at("Woc", [128, D], BF16) for s in range(2)]
        dcol = [at("dcol", [128, 4], F32) for s in range(2)]
        DMA("sync", ropec[:], ropec_d, [], ["ropec"], "ropec")
        DMA("sync", ropes[:], ropes_d, [], ["ropes"], "ropes")
        DMA("gpsimd", Wv[:], owv_d.rearrange("(kc p) n -> p kc n", p=128), [], ["Wv"], "Wv_o")
        MSET("gpsimd", V[:], 1.0, ["V"])
        for tile in range(NTILE):
            a, b_ = tile * 128, (tile + 1) * 128
            b = tile % 2
            for kc in range(KC):
                MM(ps[b][:, 0:256], nT[:, kc, a:b_], Wv[:, kc, :], kc == 0, kc == KC - 1, [("nT", st_of(tile)), "Wv"], [("ps", b)])
            CP("scalar" if b else "vector", V[:, tile, :, 0:64], ps[b][:, 0:256].rearrange("p (g d) -> p g d", g=4), [("ps", b)], ["V"])
        wq = owq_d.rearrange("(kc p) n -> p kc n", p=128)
        wk = owk_d.rearrange("(kc p) n -> p kc n", p=128)
        wcnt = [0]
        rcnt = [0]

        def rope_proj(ws, wkey, dest_fn, dkey):
            for (t0, t1) in ST_X:
                i2 = rcnt[0] % 2
                rcnt[0] += 1
                pa, pb_ = (0, 1) if i2 == 0 else (6, 7)
                for kc in range(KC):
                    MM(ps[pa][:], ws[:, kc, 0:128], nT[:, kc, t0:t1], kc == 0, kc == KC - 1, [("nT", t0), wkey], [("ps", pa)])
                for kc in range(KC):
                    MM(ps[pb_][:], ws[:, kc, 128:256], nT[:, kc, t0:t1], kc == 0, kc == KC - 1, [("nT", t0), wkey], [("ps", pb_)])
                TT("vector", r1[i2][:], ps[pa][:], ropec[:, t0 - 256:t1 - 256], ALU.mult, [("ps", pa), "ropec"], [("r1", i2)])
                TT("vector", r2[i2][:], ps[pb_][:], ropes[:, t0 - 256:t1 - 256], ALU.mult, [("ps", pb_), "ropes"], [("r2", i2)])
                TT("gpsimd", dest_fn(t0, t1), r1[i2][:], r2[i2][:], ALU.add, [("r1", i2), ("r2", i2)], [dkey])

        for g in range(4):
            s = wcnt[0] % 2
            wcnt[0] += 1
            DMA("gpsimd", Wq2[s][:], wk[:, :, g * 256:(g + 1) * 256], [], [("Wq2", s)], f"Wq2{s}")
            for kc in range(KC):
                MM(ps[2][:, 0:256], Wq2[s][:, kc, 0:128], nT[:, kc, 0:256], kc == 0, kc == KC - 1, [("nT", 0), ("Wq2", s)], [("ps", 2)])
            CP("scalar", KT[:, g, 0:256], ps[2][:, 0:256], [("ps", 2)], [("KT", g)])
            rope_proj(Wq2[s], ("Wq2", s), lambda t0, t1, g=g: KT[:, g, t0:t1], ("KT", g))
        if sub <= 1:
            p.barrier()
            return
        acnt_ = 0
        for c in range(8):
            if sub <= 5 and c >= 1:
                break
            g = c // 2
            s = wcnt[0] % 2
            wcnt[0] += 1
            cs = c % 2
            DMA("gpsimd", Wq2[s][:], wq[:, :, c * 256:(c + 1) * 256], [], [("Wq2", s)], f"Wq2{s}")
            DMA("gpsimd", Woc[cs][:], owo_d[c * 128:(c + 1) * 128, :], [], [("Woc", cs)], f"Woc{cs}")
            rope_proj(Wq2[s], ("Wq2", s), lambda t0, t1, cs=cs: QT[cs][:, t0 - 256:t1 - 256], ("QT", cs))
            def att_scores(j):
                nonlocal acnt_
                qa, qb = j * 128, (j + 1) * 128
                ktiles = [0, 1] + [kt for kt in (1 + j, 2 + j, 3 + j) if 2 <= kt <= 17]
                pts = []
                for ki, kt in enumerate(ktiles):
                    i2 = acnt_ % 3
                    acnt_ += 1
                    sb0 = 2 + 2 * i2
                    pi = (j % 2) * 5 + ki
                    for hh in range(2):
                        MM(psbig[:, sb0 + hh, 0:128], KT[hh * 64:(hh + 1) * 64, g, kt * 128:(kt + 1) * 128],
                           QT[cs][hh * 64:(hh + 1) * 64, qa:qb], True, True, [("KT", g), ("QT", cs)], [("ps", sb0 + hh)])
                    ACT(PT[pi][:].rearrange("p (h t) -> p h t", h=2), psbig[:, sb0:sb0 + 2, 0:128], AF.Exp,
                        [("ps", sb0), ("ps", sb0 + 1)], [("PTa", pi)], scale=0.125)
                    if kt >= 2 and kt != 2 + j:
                        mask, mk = (triB, "triB") if kt == 1 + j else (triF, "triF")
                        TT("gpsimd", PT[pi][:].rearrange("p (h t) -> p h t", h=2), PT[pi][:].rearrange("p (h t) -> p h t", h=2),
                           mask[:].unsqueeze(1).to_broadcast([128, 2, 128]), ALU.mult, [("PTa", pi), mk], [("PTa", pi)])
                    pts.append((pi, kt))
                return pts

            def att_pv(j, pts):
                nb = 1
                for hh in range(2):
                    for ki, (pi, kt) in enumerate(pts):
                        MM(ps[nb][:, hh * 66:hh * 66 + 65], PT[pi][:, hh * 128:(hh + 1) * 128], V[:, kt, g, 0:65], ki == 0, ki == len(pts) - 1,
                           [("PTa", pi), "V"], [("ps", nb)])
                j2 = j % 2
                dk = ("dcol", j2)
                ndv = ps[nb][:, 0:132].rearrange("p (h d) -> p h d", h=2)
                TT("vector", dcol[j2][:, 0:2], ndv[:, :, 64], esink[:, 2 * c:2 * c + 2], ALU.add, [("ps", nb), "esink"], [dk])
                RECIP(dcol[j2][:, 2:4], dcol[j2][:, 0:2], [dk], [dk])
                for hh in range(2):
                    TS("vector", Ab[j2][:, hh * 64:(hh + 1) * 64], ps[nb][:, hh * 66:hh * 66 + 64], dcol[j2][:, 2 + hh:3 + hh], None,
                       ALU.mult, None, [("ps", nb), dk], [("Ab", j2)])

            def att_tr(j):
                j2 = j % 2
                TR(psb[0][:, (j % 4) * 128:(j % 4 + 1) * 128], Ab[j2][:], ident_b[:], [("Ab", j2), "ident_b"], [("ps", 0)])
                if j % 4 == 3:
                    CP("vector", ATc[cs][:, (j - 3) * 128:(j + 1) * 128], psb[0][:, 0:512], [("ps", 0)], [("ATc", cs)])

            nxt = att_scores(0)
            for j in range(16):
                cur = nxt
                if j + 1 < 16:
                    nxt = att_scores(j + 1)
                att_pv(j, cur)
                if j >= 1:
                    att_tr(j - 1)
            att_tr(15)
            if c % 2 == 1:
                for (t0, t1) in ST_X:
                    for dc in range(KC):
                        b = dc % 2
                        for c2 in range(2):
                            MM(ps[b][:], Woc[c2][:, dc * 128:(dc + 1) * 128], ATc[c2][:, t0 - 256:t1 - 256], c2 == 0, c2 == 1,
                               [("Woc", c2), ("ATc", c2)], [("ps", b)])
                        resid(b, 512, dc, t0, t1, l, 5)
        p.barrier()

    def finish(final):
        arena_reset()
        ost = [at("ostage", [128, D], F32) for s in range(2)]
        if not final:
            for tile in range(NTILE):
                s = tile % 2
                for half in range(2):
                    b = 2 * s + half
                    for q in range(4):
                        kc = half * 4 + q
                        TR(ps[b][:, q * 128:(q + 1) * 128], hT[:, kc, tile * 128:(tile + 1) * 128], ident_f[:], [("hT", tile), "ident_f"], [("ps", b)])
                    eng = "vector" if half == 0 else "scalar"
                    CP(eng, ost[s][:, half * 512:(half + 1) * 512], ps[b][:], [("ps", b)], [("ost", s)])
                DMA("sync", out_d[tile * 128:(tile + 1) * 128, :], ost[s][:], [("ost", s)], [("outd", tile)], f"o{s}")
        else:
            fin = at("fin", [128, KC, 512], F32)
            for (t0, t1) in ST_X:
                mark = acnt[1]
                norm_phase(0, 0, [(t0, t1)], final=True, fin_out=fin)
                acnt[1] = mark
                for q4 in range(4):
                    tile = (t0 - 256) // 128 + q4
                    s = tile % 2
                    for half in range(2):
                        b = 2 * s + half
                        for q in range(4):
                            kc = half * 4 + q
                            TR(ps[b][:, q * 128:(q + 1) * 128], fin[:, kc, q4 * 128:(q4 + 1) * 128], ident_f[:], [("fin", kc), "ident_f"], [("ps", b)])
                        eng = "vector" if half == 0 else "scalar"
                        CP(eng, ost[s][:, half * 512:(half + 1) * 512], ps[b][:], [("ps", b)], [("ost", s)])
                    DMA("sync", out_d[tile * 128:(tile + 1) * 128, :], ost[s][:], [("ost", s)], [("outd", tile)], f"o{s}")
        p.add("sync", lambda e: e.nop(nofuse=True), reads=["outd"])
        p.emit()
        return nc

    arena_reset()
    load_tokens()
    modulation_head(0)
    p.barrier()
    ffn(0, 0, 0, ST_ALL, side=(0, list(range(6, 18)), (24, 72)))
    if stage <= 1:
        return finish(False)
    even_mixer(0)
    if stage <= 2:
        return finish(False)
    ffn(0, 1, 6, ST_ALL, side=(1, list(range(0, 18)), (0, 72)), end_barrier=False)
    if stage <= 3:
        p.barrier()
        return finish(False)
    ffn(1, 0, 0, ST_ALL)
    if stage <= 4:
        return finish(False)
    odd_mixer(1)
    if stage <= 5:
        return finish(False)
    ffn(1, 1, 6, ST_X)
    if stage <= 6:
        return finish(False)
    return finish(True)


def make_inputs(inputs, b):
    f = lambda a: np.ascontiguousarray(a, dtype=np.float32)
    m = {}
    m["x"] = f(inputs["x"][b])
    m["ctx"] = f(inputs["ctx"][b])
    m["cvec"] = f(np.concatenate([np.asarray(inputs["c"][b]).reshape(8, 128), np.asarray(inputs["c_ctx"]).reshape(8, 128)], 0))
    return m


def shared_inputs(inputs):
    f = lambda a: np.ascontiguousarray(a, dtype=np.float32)
    m = {}
    m["ada_w"] = f(inputs["ada_w"])
    m["ada_b"] = f(np.asarray(inputs["ada_b"]).reshape(2, 72, 128))
    m["ffn_w_in"] = f(inputs["ffn_w_in"])
    m["ffn_w_out"] = f(inputs["ffn_w_out"])
    m["even_w_in"] = f(inputs["even_w_in"][0])
    m["even_w_out"] = f(inputs["even_w_out"][0])
    ewi = np.asarray(inputs["even_w_in"][0])
    m["even_wh"] = f(np.concatenate([ewi[:, j * 512 + h * 128:j * 512 + (h + 1) * 128] for h in range(4) for j in range(4)], 1))
    m["conv"] = f(np.asarray(inputs["mlstm_conv"][0]).reshape(24, 128))
    m["gate_b"] = f(np.asarray(inputs["mlstm_gate_b"][0]).reshape(1, 16))
    m["mnorm"] = f(np.asarray(inputs["mlstm_norm"][0]).reshape(1, 512))
    m["sgun"] = f(np.asarray(inputs["sgu_norm"][0]).reshape(1, 512))
    m["sgub"] = f(np.asarray(inputs["sgu_b"][0]).reshape(1, 512))
    m["sgu_ws"] = f(inputs["sgu_ws"][0])
    wqkv = np.asarray(inputs["odd_w_qkv"][0])
    perm = np.concatenate([np.arange(0, 64, 2), np.arange(1, 64, 2)])
    swp = np.concatenate([np.arange(1, 64, 2), np.arange(0, 64, 2)])
    qn = np.concatenate([wqkv[:, h * 64 + perm] for h in range(16)], 1)
    qs = np.concatenate([wqkv[:, h * 64 + swp] for h in range(16)], 1)
    m["odd_wq"] = f(np.concatenate([np.concatenate([qn[:, c * 128:(c + 1) * 128], qs[:, c * 128:(c + 1) * 128]], 1) for c in range(8)], 1))
    kn = np.concatenate([np.concatenate([wqkv[:, 1024 + g * 64 + perm]] * 2, 1) for g in range(4)], 1)
    ks = np.concatenate([np.concatenate([wqkv[:, 1024 + g * 64 + swp]] * 2, 1) for g in range(4)], 1)
    m["odd_wk"] = f(np.concatenate([np.concatenate([kn[:, g * 128:(g + 1) * 128], ks[:, g * 128:(g + 1) * 128]], 1) for g in range(4)], 1))
    m["odd_wv"] = f(wqkv[:, 1280:1536])
    m["odd_w_out"] = f(inputs["odd_w_out"][0])
    m["sink"] = f(np.asarray(inputs["attn_sink"][0]).reshape(1, 16))
    m["fnorm"] = f(np.asarray(inputs["final_norm"]).reshape(8, 128))
    tpos = np.arange(TX)
    row = (tpos // 64).astype(np.float32)
    col = (tpos % 64).astype(np.float32)
    inv = (10000.0 ** (-np.arange(16, dtype=np.float32) / 16)).astype(np.float32)
    ang = np.concatenate([row[:, None] * inv[None, :], col[:, None] * inv[None, :]], 1).astype(np.float32)
    cs, sn = np.cos(ang).astype(np.float32), np.sin(ang).astype(np.float32)
    pidx = np.arange(128)
    ropec = cs[:, pidx % 32].T
    sign = np.where((pidx % 64) < 32, -1.0, 1.0).astype(np.float32)
    ropes = sn[:, pidx % 32].T * sign[:, None]
    m["ropec"] = f(ropec)
    m["ropes"] = f(ropes)
    return m


_STAGE = 99


def kernel(**inputs):
    nc = build_program(_STAGE)
    shared = shared_inputs(inputs)
    in_maps = []
    for b in range(8):
        m = dict(shared)
        m.update(make_inputs(inputs, b))
        in_maps.append(m)
    res = run_bass_kernel_spmd(nc, in_maps, core_ids=list(range(8)))
    return np.stack([np.asarray(r["out"], dtype=np.float32) for r in res.results], 0)
```

```python
import numpy as np
import concourse.bass as bass
import concourse.mybir as mybir
from concourse.bass_utils import run_bass_kernel_spmd

F32 = mybir.dt.float32
BF16 = mybir.dt.bfloat16
AF = mybir.ActivationFunctionType
ALU = mybir.AluOpType
AX = mybir.AxisListType

COMPUTE = ("tensor", "vector", "scalar", "gpsimd")

D = 1024
KC = 8
TCX = 256
TX = 2048
T = 2304
NTILE = 18
DFF = 2816
NFC = 22
EPS = 1e-6
ST_ALL = [(0, 256), (256, 768), (768, 1280), (1280, 1792), (1792, 2304)]
ST_X = ST_ALL[1:]
ARENA = 93 * 1024


class Op:
    __slots__ = ("eng", "fn", "deps", "is_dma", "count", "milestone", "dma_sem", "dma_count", "epoch")

    def __init__(self, eng, fn, is_dma=False):
        self.eng = eng
        self.fn = fn
        self.deps = []
        self.is_dma = is_dma
        self.milestone = False
        self.count = None
        self.dma_sem = None
        self.dma_count = None
        self.epoch = 0


class Prog:
    def __init__(self, nc):
        self.nc = nc
        self.ops = {e: [] for e in ("tensor", "vector", "scalar", "gpsimd", "sync")}
        self.state = {}
        self.dma_sems = {}
        self.epoch = 0

    def _entries(self, key):
        name, sub = key if isinstance(key, tuple) else (key, None)
        d = self.state.setdefault(name, {})
        if sub is None:
            if None not in d:
                d[None] = [None, []]
            return list(d.values())
        out = []
        if sub not in d:
            d[sub] = [None, []]
        out.append(d[sub])
        if None in d:
            out.append(d[None])
        return out

    def _add_dep(self, op, dep):
        if dep is None or dep is op:
            return
        if dep.eng == op.eng and not dep.is_dma and not op.is_dma and op.eng == "tensor":
            return
        op.deps.append(dep)

    def add(self, eng, fn, reads=(), writes=(), is_dma=False, dma_key=None):
        op = Op(eng, fn, is_dma)
        op.epoch = self.epoch
        for k in reads:
            for ent in self._entries(k):
                self._add_dep(op, ent[0])
        for k in writes:
            for ent in self._entries(k):
                self._add_dep(op, ent[0])
                for r in ent[1]:
                    self._add_dep(op, r)
        for k in reads:
            name, sub = k if isinstance(k, tuple) else (k, None)
            lst = self.state[name][sub][1]
            if not is_dma:
                lst[:] = [r for r in lst if r.is_dma or r.eng != eng]
            lst.append(op)
        for k in writes:
            name, sub = k if isinstance(k, tuple) else (k, None)
            d = self.state[name]
            if sub is None:
                for s in list(d.keys()):
                    if s is not None:
                        del d[s]
            d[sub] = [op, []]
        frozen = []
        for dep in op.deps:
            if dep.is_dma:
                frozen.append(("dma", dep.dma_sem, self.dma_sems[dep.dma_sem][1]))
            else:
                dep.milestone = True
                frozen.append(("op", dep))
        op.deps = frozen
        if is_dma:
            ds = self.dma_sems.setdefault(dma_key, [None, 0])
            ds[1] += 16
            op.dma_sem = dma_key
            op.dma_count = ds[1]
        self.ops[eng].append(op)
        return op

    def barrier(self):
        last = {}
        for e in self.ops:
            for op in reversed(self.ops[e]):
                if op.epoch != self.epoch or op.fn is None:
                    break
                if not op.is_dma and not getattr(op.fn, "_is_barrier", False):
                    last[e] = op
                    break
        dma_counts = {k: v[1] for k, v in self.dma_sems.items()}
        for e in self.ops:
            def _bnop(eng):
                return eng.nop(nofuse=True)
            _bnop._is_barrier = True
            op = Op(e, _bnop)
            op.epoch = self.epoch
            for e2, l in last.items():
                if e2 != e:
                    l.milestone = True
                    op.deps.append(("op", l))
            for k, c in dma_counts.items():
                if c > 0:
                    op.deps.append(("dma", k, c))
            self.ops[e].append(op)
        self.state = {}
        self.epoch += 1

    def emit(self):
        nc = self.nc
        sems = {}
        for k, ds in self.dma_sems.items():
            ds[0] = nc.alloc_semaphore(f"sd_{k}")
        self.maxcount = {}
        for e in self.ops:
            cnt = {}
            for op in self.ops[e]:
                if op.is_dma:
                    continue
                if op.milestone:
                    c = cnt.get(op.epoch, 0) + 1
                    cnt[op.epoch] = c
                    op.count = c
                    if (e, op.epoch) not in sems:
                        sems[(e, op.epoch)] = nc.alloc_semaphore(f"se_{e}_{op.epoch}")
            self.maxcount[e] = max(list(cnt.values()) + [0])
        assert max(self.maxcount.values()) < 8000, self.maxcount
        with nc.Block() as block:
            def make(ename):
                def body(eng):
                    known = {}
                    for op in self.ops[ename]:
                        need = {}
                        for dep in op.deps:
                            if dep[0] == "dma":
                                sem = self.dma_sems[dep[1]][0]
                                val = dep[2]
                            else:
                                sem = sems[(dep[1].eng, dep[1].epoch)]
                                val = dep[1].count
                            key = id(sem)
                            if val > need.get(key, (None, 0))[1]:
                                need[key] = (sem, val)
                        for key, (sem, val) in need.items():
                            if known.get(key, 0) >= val:
                                continue
                            known[key] = val
                            eng.wait_ge(sem, val)
                        ins = op.fn(eng)
                        if op.is_dma:
                            ins.then_inc(self.dma_sems[op.dma_sem][0], 16)
                        elif op.milestone:
                            ins.then_inc(sems[(ename, op.epoch)], 1)
                return body
            block.tensor(make("tensor"))
            block.vector(make("vector"))
            block.scalar(make("scalar"))
            block.gpsimd(make("gpsimd"))
            block.sync(make("sync"))


def build_program(stage=99, sub=99):
    nc = bass.Bass("TRN2", target_bir_lowering=False)
    p = Prog(nc)

    def din(name, shape):
        return nc.dram_tensor(name, list(shape), F32, kind="ExternalInput").ap()

    x_d = din("x", [TX, D])
    ctx_d = din("ctx", [TCX, D])
    cvec_d = din("cvec", [16, 128])
    adaw_d = din("ada_w", [2, D, 9 * D])
    adab_d = din("ada_b", [2, 72, 128])
    fwi_d = din("ffn_w_in", [2, 2, D, 2 * DFF])
    fwo_d = din("ffn_w_out", [2, 2, DFF, D])
    ewi_d = din("even_w_in", [D, 3088])
    ewo_d = din("even_w_out", [D, D])
    ewh_d = din("even_wh", [D, 2048])
    conv_d = din("conv", [24, 128])
    gateb_d = din("gate_b", [1, 16])
    mnorm_d = din("mnorm", [1, 512])
    sgun_d = din("sgun", [1, 512])
    sgub_d = din("sgub", [1, 512])
    sguws_d = din("sgu_ws", [4, 128, 128])
    owq_d = din("odd_wq", [D, 2048])
    owk_d = din("odd_wk", [D, 1024])
    owv_d = din("odd_wv", [D, 256])
    owo_d = din("odd_w_out", [D, D])
    sink_d = din("sink", [1, 16])
    fnorm_d = din("fnorm", [8, 128])
    ropec_d = din("ropec", [128, TX])
    ropes_d = din("ropes", [128, TX])
    n_out_tok = T if stage < 99 else TX
    out_d = nc.dram_tensor("out", [n_out_tok, D], F32, kind="ExternalOutput").ap()

    def sb(name, shape, dt):
        return nc.alloc_sbuf_tensor(name, list(shape), dt)

    hT = sb("hT", [128, KC, T], F32)
    nT = sb("nT", [128, KC, T], BF16)
    ident_f = sb("ident_f", [128, 128], F32)
    ones_f = sb("ones_f", [128, 128], F32)
    triF = sb("triF", [128, 128], F32)
    triB = sb("triB", [128, 128], F32)
    ident_b = sb("ident_b", [128, 128], BF16)
    ones_b = sb("ones_b", [128, 128], BF16)
    stg = sb("stg", [128, 128], F32)
    cT = sb("cT", [128, 16], F32)
    sT_b = sb("sT_b", [128, 16], BF16)
    adabT = sb("adabT", [128, 2, 72], F32)
    MOD = sb("MOD", [128, 2, 72, 2], F32)
    fnormT = sb("fnormT", [128, 8], F32)
    convT = sb("convT", [128, 24], F32)
    esink = sb("esink", [128, 16], F32)
    wsT = sb("wsT", [128, 4, 128], BF16)
    epsc = sb("epsc", [128, 1], F32)
    arena_base = (nc.sbuf_base + 63) // 64 * 64
    assert arena_base + ARENA <= nc.sbuf_top, (arena_base, nc.sbuf_top)
    acnt = [0, 0]

    def arena_reset():
        acnt[1] = 0

    def at(name, shape, dt):
        acnt[0] += 1
        esz = 4 if dt == F32 else 2
        n = 1
        for s_ in shape[1:]:
            n *= s_
        off = acnt[1]
        acnt[1] = (off + n * esz + 31) // 32 * 32
        assert acnt[1] <= ARENA, (name, off, n * esz)
        return nc.alloc_sbuf_tensor_at(f"{name}_{acnt[0]}", list(shape), dt, offset=arena_base + off)

    psbig = nc.alloc_psum_tensor("psbig", [128, 8, 512], F32)
    ps = [psbig[:, i, :] for i in range(8)]
    psb = [ps[i].bitcast(BF16) for i in range(8)]

    def MM(out, lhsT, rhs, start, stop, rd, wr):
        p.add("tensor", lambda e: e.matmul(out, lhsT, rhs, start=start, stop=stop), reads=rd, writes=wr)

    def TR(out, in_, ident, rd, wr):
        p.add("tensor", lambda e: e.transpose(out, in_, ident), reads=rd, writes=wr)

    def ACT(out, in_, func, rd, wr, bias=None, scale=None):
        kw = {}
        if bias is not None:
            kw["bias"] = bias
        if scale is not None:
            kw["scale"] = scale
        p.add("scalar", lambda e: e.activation(out=out, in_=in_, func=func, **kw), reads=rd, writes=wr)

    def TT(eng, out, in0, in1, op, rd, wr):
        p.add(eng, lambda e: e.tensor_tensor(out=out, in0=in0, in1=in1, op=op), reads=rd, writes=wr)

    def TS(eng, out, in0, s1, s2, op0, op1, rd, wr):
        if s2 is None:
            p.add(eng, lambda e: e.tensor_scalar(out=out, in0=in0, scalar1=s1, scalar2=None, op0=op0), reads=rd, writes=wr)
        else:
            p.add(eng, lambda e: e.tensor_scalar(out=out, in0=in0, scalar1=s1, scalar2=s2, op0=op0, op1=op1), reads=rd, writes=wr)

    def STT(eng, out, in0, scalar, in1, op0, op1, rd, wr):
        p.add(eng, lambda e: e.scalar_tensor_tensor(out=out, in0=in0, scalar=scalar, in1=in1, op0=op0, op1=op1), reads=rd, writes=wr)

    def CP(eng, out, in_, rd, wr):
        if eng == "scalar":
            p.add(eng, lambda e: e.copy(out=out, in_=in_), reads=rd, writes=wr)
        else:
            p.add(eng, lambda e: e.tensor_copy(out=out, in_=in_), reads=rd, writes=wr)

    def MSET(eng, ap, val, wr):
        p.add(eng, lambda e: e.memset(ap, val), writes=wr)

    def RECIP(out, in_, rd, wr):
        p.add("vector", lambda e: e.reciprocal(out=out, in_=in_), reads=rd, writes=wr)

    def RSUM(out, in_, rd, wr):
        p.add("vector", lambda e: e.reduce_sum(out=out, in_=in_, axis=AX.X), reads=rd, writes=wr)

    def DMA(q, out, in_, rd, wr, key):
        p.add(q, lambda e: e.dma_start(out=out, in_=in_), reads=rd, writes=wr, is_dma=True, dma_key=key)

    def ASEL(out, in_, pattern, cm, cmp, fill, rd, wr):
        p.add("gpsimd", lambda e: e.affine_select(out=out, in_=in_, pattern=pattern, compare_op=cmp, fill=fill, base=0, channel_multiplier=cm), reads=rd, writes=wr)

    def st_of(tile):
        return 0 if tile < 2 else 256 + ((tile - 2) // 4) * 512

    MSET("gpsimd", ones_f[:], 1.0, ["ones_f"])
    MSET("gpsimd", ones_b[:], 1.0, ["ones_b"])
    MSET("gpsimd", ident_f[:], 0.0, ["ident_f"])
    MSET("gpsimd", epsc[:], EPS, ["epsc"])
    ASEL(ident_f[:], ident_f[:], [[-1, 128]], 1, ALU.not_equal, 1.0, ["ident_f"], ["ident_f"])
    ASEL(triF[:], ones_f[:], [[1, 128]], -1, ALU.is_ge, 0.0, ["ones_f"], ["triF"])
    ASEL(triB[:], ones_f[:], [[-1, 128]], 1, ALU.is_ge, 0.0, ["ones_f"], ["triB"])
    CP("gpsimd", ident_b[:], ident_f[:], ["ident_f"], ["ident_b"])

    def load_cols(dram_ap, n, dest_ap, destkey):
        DMA("sync", stg[0:n, :], dram_ap, [], ["stg"], "stg")
        TR(ps[7][:, 0:n], stg[0:n, :], ident_f[0:n, 0:n], ["stg", "ident_f"], [("ps", 7)])
        CP("vector", dest_ap, ps[7][:, 0:n], [("ps", 7)], [destkey])

    load_cols(cvec_d, 16, cT[:], "cT")
    ACT(sT_b[:], cT[:], AF.Silu, ["cT"], ["sT_b"])
    for l in range(2):
        load_cols(adab_d[l], 72, adabT[:, l, :], ("adabT", l))
    load_cols(fnorm_d, 8, fnormT[:], "fnormT")
    load_cols(conv_d, 24, convT[:], "convT")
    DMA("sync", esink[:], sink_d[0:1, :].to_broadcast([128, 16]), [], ["esink"], "esink")
    ACT(esink[:], esink[:], AF.Exp, ["esink"], ["esink"])
    arena_reset()
    ws_stage = at("ws_stage", [128, 4, 128], F32)
    DMA("sync", ws_stage[:], sguws_d.rearrange("g p q -> p g q"), [], ["ws_stage"], "wsst")
    for g in range(4):
        TR(ps[6][:, g * 128:(g + 1) * 128], ws_stage[:, g, :], ident_f[:], ["ws_stage", "ident_f"], [("ps", 6)])
    CP("vector", wsT[:].rearrange("p g q -> p (g q)"), ps[6][:], [("ps", 6)], ["wsT"])
    p.barrier()

    def mcol(l, i, kc, j):
        return MOD[:, l, i * 8 + kc, j:j + 1]

    def modulation_gen(l, A, blocks, chunks):
        wv = adaw_d[l].rearrange("(kc p) n -> p kc n", p=128)
        for bi, blk in enumerate(blocks):
            s = bi % 2
            DMA("gpsimd", A[s][:], wv[:, :, blk * 512:(blk + 1) * 512], [], [("adaA", s)], f"adaA{s}")
            if bi >= 1:
                yield
            for n in range(4):
                col = (blk * 4 + n) * 2
                for kc in range(KC):
                    MM(ps[6][:, col:col + 2], A[s][:, kc, n * 128:(n + 1) * 128], sT_b[:, kc:16:8], kc == 0, kc == KC - 1,
                       [("adaA", s), "sT_b"], [("ps", 6)])
        c0, c1 = chunks
        mk = ("MOD", (l, c0))
        for j in range(2):
            TT("vector", MOD[:, l, c0:c1, j], ps[6][:, 2 * c0 + j:2 * c1:2], adabT[:, l, c0:c1], ALU.add, [("ps", 6), ("adabT", l)], [mk])
        for i in (1, 4, 7):
            if c0 <= i * 8 < c1:
                TS("vector", MOD[:, l, i * 8:(i + 1) * 8, :], MOD[:, l, i * 8:(i + 1) * 8, :], 1.0, None, ALU.add, None, [mk], [mk])
        for i in (2, 8):
            if c0 <= i * 8 < c1:
                TS("vector", MOD[:, l, i * 8:(i + 1) * 8, :], MOD[:, l, i * 8:(i + 1) * 8, :], 0.5, None, ALU.mult, None, [mk], [mk])
        yield

    def modulation_head(l):
        A = [at("adaA", [128, KC, 512], BF16) for s in range(2)]
        for _ in modulation_gen(l, A, list(range(0, 6)), (0, 24)):
            pass

    def load_tokens():
        xs = [at("xstage", [128, D], F32) for s in range(2)]
        for tile in range(NTILE):
            s = tile % 2
            src = ctx_d[tile * 128:(tile + 1) * 128, :] if tile < 2 else x_d[(tile - 2) * 128:(tile - 1) * 128, :]
            DMA("sync", xs[s][:], src, [], [("xs", s)], f"xs{s}")
            for half in range(2):
                b = 2 * s + half
                for q in range(4):
                    kc = half * 4 + q
                    TR(ps[b][:, q * 128:(q + 1) * 128], xs[s][:, kc * 128:(kc + 1) * 128], ident_f[:], [("xs", s), "ident_f"], [("ps", b)])
                eng = "vector" if half == 0 else "scalar"
                CP(eng, hT[:, half * 4:(half + 1) * 4, tile * 128:(tile + 1) * 128], ps[b][:].rearrange("p (q t) -> p q t", q=4),
                   [("ps", b)], [("hT", tile)])

    def norm_phase(l, i_shift, ranges, final=False, fin_out=None):
        sqf = at("sq", [128, KC, 512], BF16) if final else None
        rstd = [at("rstd", [128, 512], F32) for s in range(2)]
        tmp = [at("ntmp", [128, 512], F32) for s in range(4)]
        ranges = list(ranges)

        def sq_ap(t0, t1):
            return sqf[:, :, 0:t1 - t0] if final else nT[:, :, t0:t1]

        def sq_key(t0):
            return "sq" if final else ("nT", t0)

        def square(i):
            t0, t1 = ranges[i]
            tk = [("hT", t) for t in range(t0 // 128, t1 // 128)]
            ACT(sq_ap(t0, t1), hT[:, :, t0:t1], AF.Square, tk, [sq_key(t0)])

        square(0)
        for i, (t0, t1) in enumerate(ranges):
            n = t1 - t0
            j = 1 if t0 == 0 else 0
            r = rstd[i % 2]
            rk = ("rstd", i % 2)
            tk = [("hT", t) for t in range(t0 // 128, t1 // 128)]
            sqa = sq_ap(t0, t1)
            for kc in range(KC):
                MM(ps[7][:, 0:n], ones_b[:], sqa[:, kc, :], kc == 0, kc == KC - 1, [sq_key(t0), "ones_b"], [("ps", 7)])
            ACT(r[:, 0:n], ps[7][:, 0:n], AF.Sqrt, [("ps", 7), "epsc"], [rk], bias=epsc[:], scale=1.0 / D)
            RECIP(r[:, 0:n], r[:, 0:n], [rk], [rk])
            if i + 1 < len(ranges) and not final:
                square(i + 1)
            for kc in range(KC):
                s = kc % 4
                if final:
                    STT("vector", fin_out[:, kc, 0:n], hT[:, kc, t0:t1], fnormT[:, kc:kc + 1], r[:, 0:n], ALU.mult, ALU.mult,
                        tk + [rk, "fnormT"], [("fin", kc)])
                else:
                    STT("vector", tmp[s][:, 0:n], hT[:, kc, t0:t1], mcol(l, i_shift + 1, kc, j), r[:, 0:n], ALU.mult, ALU.mult,
                        tk + [rk, "MOD"], [("ntmp", s)])
                    if kc % 4 != 3:
                        ACT(nT[:, kc, t0:t1], tmp[s][:, 0:n], AF.Identity, [("ntmp", s), "MOD"], [("nT", t0)],
                            bias=mcol(l, i_shift, kc, j))
                    else:
                        TS("vector", nT[:, kc, t0:t1], tmp[s][:, 0:n], mcol(l, i_shift, kc, j), None, ALU.add, None,
                           [("ntmp", s), "MOD"], [("nT", t0)])
            if i + 1 < len(ranges) and final:
                square(i + 1)

    def resid(psb_, n, dc, t0, t1, l, i_gate):
        j = 1 if t0 == 0 else 0
        tk = [("hT", t) for t in range(t0 // 128, t1 // 128)]
        STT("vector", hT[:, dc, t0:t1], ps[psb_][:, 0:n], mcol(l, i_gate, dc, j), hT[:, dc, t0:t1], ALU.mult, ALU.add,
            [("ps", psb_), "MOD"] + tk, tk)

    def ffn(l, jf, i_base, ranges, side=None, end_barrier=True):
        arena_reset()
        WIg = [at("WIg", [128, KC, 512], BF16) for s in range(2)]
        WIu = [at("WIu", [128, KC, 512], BF16) for s in range(2)]
        WO = [at("WO", [128, 4, D], BF16) for s in range(2)]
        hid = [at("hid", [128, 4, 512], BF16) for s in range(2)]
        sg = [at("sg", [128, 512], BF16) for s in range(3)]
        win = fwi_d[l, jf].rearrange("(kc p) n -> p kc n", p=128)
        wout = fwo_d[l, jf].rearrange("(fc p) n -> p fc n", p=128)
        groups = [(0, 4), (4, 4), (8, 4), (12, 4), (16, 3), (19, 3)]

        def load_group(gi):
            f0, nf = groups[gi]
            s = gi % 2
            DMA("gpsimd", WIg[s][:, :, 0:nf * 128], win[:, :, f0 * 128:(f0 + nf) * 128], [], [("WIg", s)], f"wig{s}")
            DMA("gpsimd", WIu[s][:, :, 0:nf * 128], win[:, :, DFF + f0 * 128:DFF + (f0 + nf) * 128], [], [("WIu", s)], f"wiu{s}")
            DMA("gpsimd", WO[s][:, 0:nf, :], wout[:, f0:f0 + nf, :], [], [("WO", s)], f"wo{s}")

        fcnt = [0]

        def GU(gi, t0, t1, hs):
            f0, nf = groups[gi]
            s = gi % 2
            n = t1 - t0
            for fi in range(nf):
                k5 = fcnt[0]
                fcnt[0] += 1
                bg_, bu_ = (2 * k5) % 5, (2 * k5 + 1) % 5
                b = k5 % 3
                for kc in range(KC):
                    MM(ps[bg_][:, 0:n], WIg[s][:, kc, fi * 128:(fi + 1) * 128], nT[:, kc, t0:t1], kc == 0, kc == KC - 1,
                       [("WIg", s), ("nT", t0)], [("ps", bg_)])
                for kc in range(KC):
                    MM(ps[bu_][:, 0:n], WIu[s][:, kc, fi * 128:(fi + 1) * 128], nT[:, kc, t0:t1], kc == 0, kc == KC - 1,
                       [("WIu", s), ("nT", t0)], [("ps", bu_)])
                ACT(sg[b][:, 0:n], ps[bg_][:, 0:n], AF.Silu, [("ps", bg_)], [("sg", b)])
                TT("vector", hid[hs][:, fi, 0:n], sg[b][:, 0:n], ps[bu_][:, 0:n], ALU.mult, [("sg", b), ("ps", bu_)], [("hid", (hs, fi))])

        def WOUT(gi, t0, t1, hs):
            f0, nf = groups[gi]
            s = gi % 2
            n = t1 - t0
            for dc in range(KC):
                b = 5 if dc % 2 == 0 else 7
                for fi in range(nf):
                    MM(ps[b][:, 0:n], WO[s][:, fi, dc * 128:(dc + 1) * 128], hid[hs][:, fi, 0:n], fi == 0, fi == nf - 1,
                       [("WO", s), ("hid", (hs, fi))], [("ps", b)])
                resid(b, n, dc, t0, t1, l, i_base + 2)

        load_group(0)
        load_group(1)
        norm_phase(l, i_base, ranges)
        items = [(gi, t0, t1) for gi in range(len(groups)) for (t0, t1) in ranges]
        side_gen = None
        if side is not None:
            A = [at("adaA", [128, KC, 512], BF16) for s in range(2)]
            side_gen = modulation_gen(side[0], A, side[1], side[2])
        prev = None
        for k, (gi, t0, t1) in enumerate(items):
            GU(gi, t0, t1, k % 2)
            if prev is not None:
                WOUT(*prev)
                if prev[0] != gi and gi + 1 < len(groups):
                    load_group(gi + 1)
            prev = (gi, t0, t1, k % 2)
            if side_gen is not None and k >= 1:
                try:
                    next(side_gen)
                except StopIteration:
                    side_gen = None
        WOUT(*prev)
        if side_gen is not None:
            for _ in side_gen:
                pass
        if end_barrier:
            p.barrier()

    def even_mixer(l):
        arena_reset()
        norm_phase(l, 3, ST_ALL)
        Wu = at("Wu", [128, KC, 512], BF16)
        Wv = at("Wv", [128, KC, 512], BF16)
        HBT = at("HBT", [128, 4, T], BF16)
        sgun_bc = at("sgun_bc", [128, 512], F32)
        sgub_bc = at("sgub_bc", [128, 512], F32)
        Wo4 = at("Wo4", [128, 4, D], BF16)
        vt = [at("vt", [128, 512], F32) for s in range(2)]
        ut = [at("ut", [128, 512], F32) for s in range(2)]
        vn = [at("vn", [128, 512], BF16) for s in range(2)]
        sqj = at("sqj", [128, 512], F32)
        mt = [at("mt", [128, 512], F32) for s in range(2)]
        scol = [at("scol", [128, 4], F32) for s in range(2)]
        wi = ewi_d.rearrange("(kc p) n -> p kc n", p=128)
        DMA("gpsimd", Wu[:], wi[:, :, 2064:2576], [], ["Wu"], "Wu")
        DMA("gpsimd", Wv[:], wi[:, :, 2576:3088], [], ["Wv"], "Wv_e")
        DMA("gpsimd", Wo4[:], ewo_d[512:1024, :].rearrange("(g p) n -> p g n", p=128), [], ["Wo4"], "Wo4")
        DMA("sync", sgun_bc[:], sgun_d[0:1, :].to_broadcast([128, 512]), [], ["sgun_bc"], "sgun")
        DMA("sync", sgub_bc[:], sgub_d[0:1, :].to_broadcast([128, 512]), [], ["sgub_bc"], "sgub")
        vg = at("vg", [128, NTILE, 512], BF16)
        ssq = at("ssq", [128, 3, NTILE], F32)
        for tile in range(NTILE):
            a, b_ = tile * 128, (tile + 1) * 128
            s = tile % 2
            b0 = 3 * s
            nk = ("nT", st_of(tile))
            for kc in range(KC):
                MM(ps[b0][:], nT[:, kc, a:b_], Wv[:, kc, :], kc == 0, kc == KC - 1, [nk, "Wv"], [("ps", b0)])
            ACT(vg[:, tile, :], ps[b0][:], AF.Gelu_apprx_tanh, [("ps", b0)], [("vg", tile)])
            TT("vector", vt[s][:], vg[:, tile, :], vg[:, tile, :], ALU.mult, [("vg", tile)], [("vt", s)])
            RSUM(ssq[:, 0, tile:tile + 1], vt[s][:], [("vt", s)], [("ssq", tile)])

        def sgu_u(tile):
            a, b_ = tile * 128, (tile + 1) * 128
            s = tile % 2
            b1 = 3 * s + 1
            nk = ("nT", st_of(tile))
            for g in range(4):
                for kc in range(KC):
                    MM(ps[b1][:, g * 128:(g + 1) * 128], Wu[:, kc, g * 128:(g + 1) * 128], nT[:, kc, a:b_], kc == 0, kc == KC - 1,
                       [nk, "Wu"], [("ps", b1)])
            ACT(ut[s][:], ps[b1][:], AF.Gelu_apprx_tanh, [("ps", b1)], [("ut", s)])

        sgu_u(0)
        ACT(ssq[:, 1, :], ssq[:, 0, :], AF.Sqrt, ["ssq", "epsc"], ["ssq1"], bias=epsc[:], scale=1.0 / 512)
        RECIP(ssq[:, 2, :], ssq[:, 1, :], ["ssq1"], ["ssq2"])
        for tile in range(NTILE):
            a, b_ = tile * 128, (tile + 1) * 128
            s = tile % 2
            b2 = 3 * s + 2
            if tile + 1 < NTILE:
                sgu_u(tile + 1)
            STT("vector", vn[s][:], vg[:, tile, :], ssq[:, 2, tile:tile + 1], sgun_bc[:], ALU.mult, ALU.mult,
                [("vg", tile), "ssq2", "sgun_bc"], [("vn", s)])
            for g in range(4):
                MM(ps[b2][:, g * 128:(g + 1) * 128], vn[s][:, g * 128:(g + 1) * 128], wsT[:, g, :], True, True, [("vn", s), "wsT"], [("ps", b2)])
            TT("vector", mt[s][:], ps[b2][:], sgub_bc[:], ALU.add, [("ps", b2), "sgub_bc"], [("mt", s)])
            TT("gpsimd", HBT[:, :, a:b_], mt[s][:].rearrange("p (g t) -> p g t", g=4), ut[s][:].rearrange("p (g t) -> p g t", g=4), ALU.mult,
               [("mt", s), ("ut", s)], [("HBT", tile)])
        for (t0, t1) in ST_ALL:
            n = t1 - t0
            tks = [("HBT", t) for t in range(t0 // 128, t1 // 128)]
            for dc in range(KC):
                b = 6 + dc % 2
                for g in range(4):
                    MM(ps[b][:, 0:n], Wo4[:, g, dc * 128:(dc + 1) * 128], HBT[:, g, t0:t1], g == 0, g == 3, ["Wo4"] + tks, [("ps", b)])
                resid(b, n, dc, t0, t1, l, 5)
        p.barrier()
        arena_reset()
        Wh = [at("Wh", [128, KC, 512], BF16)] * 2
        Woh = [at("Woh", [128, D], BF16) for s in range(2)]
        pre = at("pre", [128, T], F32)
        qT = at("qT", [128, T], BF16)
        kT = at("kT", [128, T], BF16)
        kTok = at("kTok", [128, NTILE, 128], BF16)
        vaug = at("vaug", [128, NTILE, 130], BF16)
        og = at("og", [128, NTILE, 128], BF16)
        HF = at("HF", [128, NTILE, 128], BF16)
        Cb = [at("Cb", [128, NTILE, 130], BF16) for d in range(2)]
        HAT = at("HAT", [128, T], BF16)
        mnorm_bc = at("mnorm_bc", [128, 512], F32)
        EA = [at("EA", [128, NTILE, 4], F32) for d in range(2)]
        EB = [at("EB", [128, NTILE, 4], F32) for d in range(2)]
        EG = [at("EG", [128, NTILE, 4], F32) for d in range(2)]
        C32 = [at("C32", [128, 130], F32) for d in range(2)]
        C32b = [at("C32b", [128, 130], F32) for d in range(2)]
        cacc = [at("cacc", [128, 512], F32) for s in range(2)]
        ktl = [at("ktl", [128, 128], BF16) for s in range(4)]
        PT = [at("PT", [128, 128], BF16) for s in range(3)]
        NDs = at("NDs", [128, NTILE, 130], F32)
        rc = at("rc", [128, 8, NTILE], F32)
        sc = [at("sc", [128, 8], F32) for s in range(2)]
        og2 = [og, at("og2", [128, NTILE, 128], BF16)]
        gate_mark = acnt[1]
        Wg = at("Wg", [128, KC, 16], BF16)
        G = at("G", [128, NTILE, 16], F32)
        LF = at("LF", [128, NTILE, 16], F32)
        gb_bc = at("gb_bc", [128, 16], F32)
        DMA("gpsimd", Wg[:], wi[:, :, 2048:2064], [], ["Wg"], "Wg")
        DMA("sync", mnorm_bc[:], mnorm_d[0:1, :].to_broadcast([128, 512]), [], ["mnorm_bc"], "mnorm")
        DMA("sync", gb_bc[:], gateb_d[0:1, :].to_broadcast([128, 16]), [], ["gb_bc"], "gb")
        MSET("gpsimd", vaug[:], 1.0, ["vaug"])
        for tile in range(NTILE):
            a, b_ = tile * 128, (tile + 1) * 128
            for kc in range(KC):
                MM(ps[0][:, tile * 16:(tile + 1) * 16], nT[:, kc, a:b_], Wg[:, kc, :], kc == 0, kc == KC - 1, [("nT", st_of(tile)), "Wg"], [("ps", 0)])
        v3 = lambda ap: ap.rearrange("p (t g) -> p t g", g=16)
        TT("vector", G[:], v3(ps[0][:, 0:288]), gb_bc[:].unsqueeze(1).to_broadcast([128, NTILE, 16]), ALU.add, [("ps", 0), "gb_bc"], ["G"])
        ACT(LF[:], G[:], AF.Exp, ["G"], ["LF"], scale=-1.0)
        ACT(LF[:], LF[:], AF.Ln, ["LF"], ["LF"], bias=1.0)
        TS("vector", LF[:], LF[:], -1.0, None, ALU.mult, None, ["LF"], ["LF"])
        LF2 = LF[:].rearrange("p t g -> p (t g)")
        MM(ps[1][:, 0:288], triF[:], LF2, True, True, ["triF", "LF"], [("ps", 1)])
        MM(ps[2][:, 0:288], triB[:], LF2, True, True, ["triB", "LF"], [("ps", 2)])
        MM(ps[3][:, 0:288], ones_f[:], LF2, True, True, ["ones_f", "LF"], [("ps", 3)])
        LNS = float(np.log(128.0 ** -0.5))
        lnsc = sc[0][:, 7:8]
        MSET("vector", lnsc, LNS, [("sc", 0)])
        for d in range(2):
            igv = G[:, :, 0:4] if d == 0 else G[:, :, 8:12]
            bv = v3(ps[1][:, 0:288])[:, :, 4:8] if d == 0 else v3(ps[2][:, 0:288])[:, :, 12:16]
            gv = v3(ps[3][:, 0:288])[:, :, 4:8] if d == 0 else v3(ps[3][:, 0:288])[:, :, 12:16]
            pk = ("ps", 1 if d == 0 else 2)
            TT("vector", EA[d][:], igv, bv, ALU.subtract, ["G", pk], [("EA", d)])
            ACT(EA[d][:], EA[d][:], AF.Exp, [("EA", d), ("sc", 0)], [("EA", d)], bias=lnsc)
            ACT(EB[d][:], bv, AF.Exp, [pk], [("EB", d)])
            ACT(EG[d][:], gv, AF.Exp, [("ps", 3)], [("EG", d)])
        p.barrier()
        acnt[1] = gate_mark
        HAT2 = [HAT, at("HATb", [128, T], BF16)]
        pcnt = [0]
        pre3 = pre[:, 0:T].rearrange("p (c e) -> p c e", e=128)
        HAT3 = HAT[:, 0:T].rearrange("p (c e) -> p c e", e=128)
        orders = [list(range(NTILE)), [1, 0] + list(range(NTILE - 1, 1, -1))]

        def run(*gens):
            gens = [g for g in gens if g is not None]
            while gens:
                for g_ in list(gens):
                    try:
                        next(g_)
                    except StopIteration:
                        gens.remove(g_)

        def proj_fm(wcols, dest, cidx):
            for (t0, t1) in ST_ALL:
                n = t1 - t0
                b = pcnt[0] % 2
                pcnt[0] += 1
                for kc in range(KC):
                    MM(ps[b][:, 0:n], wcols[:, kc, :], nT[:, kc, t0:t1], kc == 0, kc == KC - 1, [("nT", t0), ("Wh", 0)], [("ps", b)])
                CP("scalar", pre[:, t0:t1], ps[b][:, 0:n], [("ps", b)], [("pre", t0)])
                yield
            for si, (t0, t1) in enumerate(ST_ALL):
                n = t1 - t0
                s0, s1 = (0, 256) if t0 == 0 else (256, T)
                ca = cacc[pcnt[0] % 2]
                ck = ("cacc", pcnt[0] % 2)
                pcnt[0] += 1
                pk = [("pre", ST_ALL[i][0]) for i in (si - 1, si, si + 1) if 0 <= i < len(ST_ALL)]
                w = lambda tap: convT[:, tap * 8 + cidx:tap * 8 + cidx + 1]
                ACT(ca[:, 0:n], pre[:, t0:t1], AF.Identity, pk + ["convT"], [ck], scale=w(1))
                lo = max(t0, s0 + 1)
                STT("vector", ca[:, lo - t0:n], pre[:, lo - 1:t1 - 1], w(0), ca[:, lo - t0:n], ALU.mult, ALU.add, pk + ["convT", ck], [ck])
                hi = min(t1, s1 - 1)
                STT("vector", ca[:, 0:hi - t0], pre[:, t0 + 1:hi + 1], w(2), ca[:, 0:hi - t0], ALU.mult, ALU.add, pk + ["convT", ck], [ck])
                ACT(dest[:, t0:t1], ca[:, 0:n], AF.Silu, [ck], [dest.name])
                yield

        def P(h):
            s = h % 2
            ogh = og2[s]
            yield from proj_fm(Wh[s][:, :, 0:128], qT, h)
            for t4 in range(0, NTILE, 4):
                nt = min(4, NTILE - t4)
                for q in range(nt):
                    tile = t4 + q
                    for kc in range(KC):
                        MM(ps[3][:, q * 128:(q + 1) * 128], nT[:, kc, tile * 128:(tile + 1) * 128], Wh[s][:, kc, 256:384], kc == 0, kc == KC - 1,
                           [("nT", st_of(tile)), ("Wh", 0)], [("ps", 3)])
                CP("scalar", vaug[:, t4:t4 + nt, 0:128], ps[3][:, 0:nt * 128].rearrange("p (q t) -> p q t", q=nt), [("ps", 3)], ["vaug"])
                yield
                for q in range(nt):
                    tile = t4 + q
                    for kc in range(KC):
                        MM(ps[4][:, q * 128:(q + 1) * 128], nT[:, kc, tile * 128:(tile + 1) * 128], Wh[s][:, kc, 384:512], kc == 0, kc == KC - 1,
                           [("nT", st_of(tile)), ("Wh", 0)], [("ps", 4)])
                ACT(ogh[:, t4:t4 + nt, :], ps[4][:, 0:nt * 128].rearrange("p (q t) -> p q t", q=nt), AF.Sigmoid, [("ps", 4)], [("og", s)])
                yield
            yield from proj_fm(Wh[s][:, :, 128:256], kT, 4 + h)
            for t4 in range(0, NTILE, 4):
                nt = min(4, NTILE - t4)
                for q in range(nt):
                    tile = t4 + q
                    TR(psb[2][:, q * 128:(q + 1) * 128], kT[:, tile * 128:(tile + 1) * 128], ident_b[:], [kT.name, "ident_b"], [("ps", 2)])
                CP("vector", kTok[:, t4:t4 + nt, :], psb[2][:, 0:nt * 128].rearrange("p (q t) -> p q t", q=nt), [("ps", 2)], ["kTok"])
                yield

        scnt = [0]

        def S(h, d):
            order = orders[d]
            Ca = [C32[d], C32b[d]]
            kt2 = ktl[2 * d:2 * d + 2]
            MSET("vector", Ca[0][:], 0.0, [("C32", (d, 0))])
            MSET("vector", Cb[d][:, order[0], :], 0.0, [("Cb", (d, order[0]))])

            def mk_ktl(idx):
                c = order[idx]
                i2 = idx % 2
                ACT(kt2[i2][:], kTok[:, c, :], AF.Identity, ["kTok", ("EA", d)], [("ktl", (d, i2))], scale=EA[d][:, c, h:h + 1])

            mk_ktl(0)
            for idx, c in enumerate(order[:-1]):
                i2 = idx % 2
                pb = 5 + i2
                co, cn = Ca[idx % 2], Ca[(idx + 1) % 2]
                ko, kn = ("C32", (d, idx % 2)), ("C32", (d, (idx + 1) % 2))
                cprev = order[idx - 1] if idx >= 1 else c
                MM(ps[pb][:, d * 256:d * 256 + 129], kt2[i2][:], vaug[:, c, 0:129], True, True, [("ktl", (d, i2)), "vaug"], [("ps", pb)])
                STT("vector", cn[:, 0:129], co[:, 0:129], EG[d][:, cprev, h:h + 1], ps[pb][:, d * 256:d * 256 + 129], ALU.mult, ALU.add,
                    [ko, ("EG", d), ("ps", pb)], [kn])
                if idx + 1 < len(order) - 1:
                    mk_ktl(idx + 1)
                ACT(Cb[d][:, order[idx + 1], 0:129], cn[:, 0:129], AF.Identity, [kn, ("EG", d)], [("Cb", (d, order[idx + 1]))],
                    scale=EG[d][:, c, h:h + 1])
                yield

        def O(h, d):
            mask = triF if d == 0 else triB
            mk = "triF" if d == 0 else "triB"
            order = orders[d]

            SB = (0, 1, 4)
            NB = (2, 3, 7)

            def st_mm(c, i3):
                a, b_ = c * 128, (c + 1) * 128
                sb_ = SB[i3]
                MM(ps[sb_][:, 0:128], kT[:, a:b_], qT[:, a:b_], True, True, [kT.name, qT.name], [("ps", sb_)])
                STT("vector", PT[i3][:], ps[sb_][:, 0:128], EA[d][:, c, h:h + 1], mask[:], ALU.mult, ALU.mult, [("ps", sb_), ("EA", d), mk], [("PT", i3)])

            def nd_mm(c, i3):
                a, b_ = c * 128, (c + 1) * 128
                nb_ = NB[i3]
                MM(ps[nb_][:, 0:129], PT[i3][:], vaug[:, c, 0:129], True, False, [("PT", i3), "vaug"], [("ps", nb_)])
                MM(ps[nb_][:, 0:129], qT[:, a:b_], Cb[d][:, c, 0:129], False, True, [qT.name, ("Cb", (d, c))], [("ps", nb_)])
                CP("scalar", NDs[:, c, 0:129], ps[nb_][:, 0:129], [("ps", nb_)], [("NDs", c)])

            st_mm(order[0], 0)
            if NTILE > 1:
                st_mm(order[1], 1)
            for k, c in enumerate(order):
                if k + 2 < NTILE:
                    st_mm(order[k + 2], (k + 2) % 3)
                nd_mm(c, k % 3)
                yield
            ebv = EB[d][:, :, h]
            TT("vector", rc[:, 0, :], NDs[:, :, 128], ebv, ALU.mult, ["NDs", ("EB", d)], ["rc"])
            ACT(rc[:, 1, :], rc[:, 0, :], AF.Abs, ["rc"], ["rc"])
            TS("vector", rc[:, 1, :], rc[:, 1, :], 1.0, None, ALU.max, None, ["rc"], ["rc"])
            RECIP(rc[:, 2, :], rc[:, 1, :], ["rc"], ["rc"])
            TT("vector", rc[:, 3, :], rc[:, 2, :], ebv, ALU.mult, ["rc", ("EB", d)], ["rc"])
            yield
            r_bc = rc[:, 3, :].unsqueeze(2).to_broadcast([128, NTILE, 128])
            ndv = NDs[:, :, 0:128]
            if d == 0:
                TT("vector", HF[:], ndv, r_bc, ALU.mult, ["NDs", "rc"], ["HF"])
            else:
                TT("vector", ndv, ndv, r_bc, ALU.mult, ["NDs", "rc"], ["NDs"])
            yield

        def NW(h):
            s = h % 2
            HATs = HAT2[s]
            HAT3 = HATs[:, 0:T].rearrange("p (c e) -> p c e", e=128)
            hk = ("HAT", s)
            ndv = NDs[:, :, 0:128]
            TT("vector", ndv, ndv, HF[:], ALU.add, ["NDs", "HF"], ["NDs"])
            yield
            ACT(HAT3, ndv, AF.Square, ["NDs"], [hk])
            yield
            RSUM(rc[:, 4, :], HAT3, [hk], ["rc"])
            ACT(rc[:, 5, :], rc[:, 4, :], AF.Sqrt, ["rc", "epsc"], ["rc"], bias=epsc[:], scale=1.0 / 128)
            RECIP(rc[:, 6, :], rc[:, 5, :], ["rc"], ["rc"])
            yield
            TT("vector", ndv, ndv, rc[:, 6, :].unsqueeze(2).to_broadcast([128, NTILE, 128]), ALU.mult, ["NDs", "rc"], ["NDs"])
            yield
            TT("vector", ndv, ndv, mnorm_bc[:, h * 128:(h + 1) * 128].unsqueeze(1).to_broadcast([128, NTILE, 128]), ALU.mult,
               ["NDs", "mnorm_bc"], ["NDs"])
            yield
            TT("vector", HF[:], ndv, og2[s][:], ALU.mult, ["NDs", ("og", s)], ["HF"])
            yield
            for t4 in range(0, NTILE, 4):
                nt = min(4, NTILE - t4)
                for q in range(nt):
                    TR(psb[5][:, q * 128:(q + 1) * 128], HF[:, t4 + q, :], ident_b[:], ["HF", "ident_b"], [("ps", 5)])
                CP("scalar", HATs[:, t4 * 128:(t4 + nt) * 128], psb[5][:, 0:nt * 128], [("ps", 5)], [hk])
                yield
            if h % 2 == 1:
                for (t0, t1) in ST_ALL:
                    n = t1 - t0
                    for dc in range(KC):
                        b = 6 + dc % 2
                        for h2 in range(2):
                            MM(ps[b][:, 0:n], Woh[h2][:, dc * 128:(dc + 1) * 128], HAT2[h2][:, t0:t1], h2 == 0, h2 == 1,
                               [("Woh", h2), ("HAT", h2)], [("ps", b)])
                        resid(b, n, dc, t0, t1, l, 5)
                        if dc % 4 == 3:
                            yield

        wih = ewh_d.rearrange("(kc p) n -> p kc n", p=128)

        def load_head(h):
            s = h % 2
            DMA("gpsimd", Wh[s][:], wih[:, :, h * 512:(h + 1) * 512], [], [("Wh", 0)], "Wh0")

        def load_woh(hp):
            for h2 in range(2):
                DMA("gpsimd", Woh[h2][:], ewo_d[(2 * hp + h2) * 128:(2 * hp + h2 + 1) * 128, :], [], [("Woh", h2)], f"Woh{h2}")

        load_head(0)
        load_woh(0)
        run(P(0))
        load_head(1)
        for h in range(4):
            run(S(h, 0), S(h, 1), O(h, 0))
            run(O(h, 1))
            run(NW(h), P(h + 1) if h + 1 < 4 else None)
            if h + 2 < 4:
                load_head(h + 2)
            if h == 1:
                load_woh(1)
        p.barrier()

    def odd_mixer(l):
        arena_reset()
        norm_phase(l, 3, ST_ALL)
        p.barrier()
        arena_reset()
        KT = at("KT", [128, 4, T], BF16)
        V = at("V", [128, NTILE, 4, 66], BF16)
        ropec = at("ropec", [128, TX], F32)
        ropes = at("ropes", [128, TX], F32)
        Wq2 = [at("Wq2", [128, KC, 256], BF16) for s in range(2)]
        Wv = at("Wv", [128, KC, 256], BF16)
        QT = [at("QT", [128, TX], BF16) for s in range(2)]
        r1 = [at("r1", [128, 512], F32) for s in range(2)]
        r2 = [at("r2", [128, 512], F32) for s in range(2)]
        PT = [at("PTa", [128, 256], BF16) for s in range(10)]
        Ab = [at("Ab", [128, 128], BF16) for s in range(2)]
        ATc = [at("ATc", [128, TX], BF16) for s in range(2)]
        Woc = [at("Woc", [128, D], BF16) for s in range(2)]
        dcol = [at("dcol", [128, 4], F32) for s in range(2)]
        DMA("sync", ropec[:], ropec_d, [], ["ropec"], "ropec")
        DMA("sync", ropes[:], ropes_d, [], ["ropes"], "ropes")
        DMA("gpsimd", Wv[:], owv_d.rearrange("(kc p) n -> p kc n", p=128), [], ["Wv"], "Wv_o")
        MSET("gpsimd", V[:], 1.0, ["V"])
        for tile in range(NTILE):
            a, b_ = tile * 128, (tile + 1) * 128
            b = tile % 2
            for kc in range(KC):
                MM(ps[b][:, 0:256], nT[:, kc, a:b_], Wv[:, kc, :], kc == 0, kc == KC - 1, [("nT", st_of(tile)), "Wv"], [("ps", b)])
            CP("scalar" if b else "vector", V[:, tile, :, 0:64], ps[b][:, 0:256].rearrange("p (g d) -> p g d", g=4), [("ps", b)], ["V"])
        wq = owq_d.rearrange("(kc p) n -> p kc n", p=128)
        wk = owk_d.rearrange("(kc p) n -> p kc n", p=128)
        wcnt = [0]
        rcnt = [0]

        def rope_proj(ws, wkey, dest_fn, dkey):
            for (t0, t1) in ST_X:
                i2 = rcnt[0] % 2
                rcnt[0] += 1
                pa, pb_ = (0, 1) if i2 == 0 else (6, 7)
                for kc in range(KC):
                    MM(ps[pa][:], ws[:, kc, 0:128], nT[:, kc, t0:t1], kc == 0, kc == KC - 1, [("nT", t0), wkey], [("ps", pa)])
                for kc in range(KC):
                    MM(ps[pb_][:], ws[:, kc, 128:256], nT[:, kc, t0:t1], kc == 0, kc == KC - 1, [("nT", t0), wkey], [("ps", pb_)])
                TT("vector", r1[i2][:], ps[pa][:], ropec[:, t0 - 256:t1 - 256], ALU.mult, [("ps", pa), "ropec"], [("r1", i2)])
                TT("vector", r2[i2][:], ps[pb_][:], ropes[:, t0 - 256:t1 - 256], ALU.mult, [("ps", pb_), "ropes"], [("r2", i2)])
                TT("gpsimd", dest_fn(t0, t1), r1[i2][:], r2[i2][:], ALU.add, [("r1", i2), ("r2", i2)], [dkey])

        for g in range(4):
            s = wcnt[0] % 2
            wcnt[0] += 1
            DMA("gpsimd", Wq2[s][:], wk[:, :, g * 256:(g + 1) * 256], [], [("Wq2", s)], f"Wq2{s}")
            for kc in range(KC):
                MM(ps[2][:, 0:256], Wq2[s][:, kc, 0:128], nT[:, kc, 0:256], kc == 0, kc == KC - 1, [("nT", 0), ("Wq2", s)], [("ps", 2)])
            CP("scalar", KT[:, g, 0:256], ps[2][:, 0:256], [("ps", 2)], [("KT", g)])
            rope_proj(Wq2[s], ("Wq2", s), lambda t0, t1, g=g: KT[:, g, t0:t1], ("KT", g))
        if sub <= 1:
            p.barrier()
            return
        acnt_ = 0
        for c in range(8):
            if sub <= 5 and c >= 1:
                break
            g = c // 2
            s = wcnt[0] % 2
            wcnt[0] += 1
            cs = c % 2
            DMA("gpsimd", Wq2[s][:], wq[:, :, c * 256:(c + 1) * 256], [], [("Wq2", s)], f"Wq2{s}")
            DMA("gpsimd", Woc[cs][:], owo_d[c * 128:(c + 1) * 128, :], [], [("Woc", cs)], f"Woc{cs}")
            rope_proj(Wq2[s], ("Wq2", s), lambda t0, t1, cs=cs: QT[cs][:, t0 - 256:t1 - 256], ("QT", cs))
            def att_scores(j):
                nonlocal acnt_
                qa, qb = j * 128, (j + 1) * 128
                ktiles = [0, 1] + [kt for kt in (1 + j, 2 + j, 3 + j) if 2 <= kt <= 17]
                pts = []
                for ki, kt in enumerate(ktiles):
                    i2 = acnt_ % 3
                    acnt_ += 1
                    sb0 = 2 + 2 * i2
                    pi = (j % 2) * 5 + ki
                    for hh in range(2):
                        MM(psbig[:, sb0 + hh, 0:128], KT[hh * 64:(hh + 1) * 64, g, kt * 128:(kt + 1) * 128],
                           QT[cs][hh * 64:(hh + 1) * 64, qa:qb], True, True, [("KT", g), ("QT", cs)], [("ps", sb0 + hh)])
                    ACT(PT[pi][:].rearrange("p (h t) -> p h t", h=2), psbig[:, sb0:sb0 + 2, 0:128], AF.Exp,
                        [("ps", sb0), ("ps", sb0 + 1)], [("PTa", pi)], scale=0.125)
                    if kt >= 2 and kt != 2 + j:
                        mask, mk = (triB, "triB") if kt == 1 + j else (triF, "triF")
                        TT("gpsimd", PT[pi][:].rearrange("p (h t) -> p h t", h=2), PT[pi][:].rearrange("p (h t) -> p h t", h=2),
                           mask[:].unsqueeze(1).to_broadcast([128, 2, 128]), ALU.mult, [("PTa", pi), mk], [("PTa", pi)])
                    pts.append((pi, kt))
                return pts

            def att_pv(j, pts):
                nb = 1
                for hh in range(2):
                    for ki, (pi, kt) in enumerate(pts):
                        MM(ps[nb][:, hh * 66:hh * 66 + 65], PT[pi][:, hh * 128:(hh + 1) * 128], V[:, kt, g, 0:65], ki == 0, ki == len(pts) - 1,
                           [("PTa", pi), "V"], [("ps", nb)])
                j2 = j % 2
                dk = ("dcol", j2)
                ndv = ps[nb][:, 0:132].rearrange("p (h d) -> p h d", h=2)
                TT("vector", dcol[j2][:, 0:2], ndv[:, :, 64], esink[:, 2 * c:2 * c + 2], ALU.add, [("ps", nb), "esink"], [dk])
                RECIP(dcol[j2][:, 2:4], dcol[j2][:, 0:2], [dk], [dk])
                for hh in range(2):
                    TS("vector", Ab[j2][:, hh * 64:(hh + 1) * 64], ps[nb][:, hh * 66:hh * 66 + 64], dcol[j2][:, 2 + hh:3 + hh], None,
                       ALU.mult, None, [("ps", nb), dk], [("Ab", j2)])

            def att_tr(j):
                j2 = j % 2
                TR(psb[0][:, (j % 4) * 128:(j % 4 + 1) * 128], Ab[j2][:], ident_b[:], [("Ab", j2), "ident_b"], [("ps", 0)])
                if j % 4 == 3:
                    CP("vector", ATc[cs][:, (j - 3) * 128:(j + 1) * 128], psb[0][:, 0:512], [("ps", 0)], [("ATc", cs)])

            nxt = att_scores(0)
            for j in range(16):
                cur = nxt
                if j + 1 < 16:
                    nxt = att_scores(j + 1)
                att_pv(j, cur)
                if j >= 1:
                    att_tr(j - 1)
            att_tr(15)
            if c % 2 == 1:
                for (t0, t1) in ST_X:
                    for dc in range(KC):
                        b = dc % 2
                        for c2 in range(2):
                            MM(ps[b][:], Woc[c2][:, dc * 128:(dc + 1) * 128], ATc[c2][:, t0 - 256:t1 - 256], c2 == 0, c2 == 1,
                               [("Woc", c2), ("ATc", c2)], [("ps", b)])
                        resid(b, 512, dc, t0, t1, l, 5)
        p.barrier()

    def finish(final):
        arena_reset()
        ost = [at("ostage", [128, D], F32) for s in range(2)]
        if not final:
            for tile in range(NTILE):
                s = tile % 2
                for half in range(2):
                    b = 2 * s + half
                    for q in range(4):
                        kc = half * 4 + q
                        TR(ps[b][:, q * 128:(q + 1) * 128], hT[:, kc, tile * 128:(tile + 1) * 128], ident_f[:], [("hT", tile), "ident_f"], [("ps", b)])
                    eng = "vector" if half == 0 else "scalar"
                    CP(eng, ost[s][:, half * 512:(half + 1) * 512], ps[b][:], [("ps", b)], [("ost", s)])
                DMA("sync", out_d[tile * 128:(tile + 1) * 128, :], ost[s][:], [("ost", s)], [("outd", tile)], f"o{s}")
        else:
            fin = at("fin", [128, KC, 512], F32)
            for (t0, t1) in ST_X:
                mark = acnt[1]
                norm_phase(0, 0, [(t0, t1)], final=True, fin_out=fin)
                acnt[1] = mark
                for q4 in range(4):
                    tile = (t0 - 256) // 128 + q4
                    s = tile % 2
                    for half in range(2):
                        b = 2 * s + half
                        for q in range(4):
                            kc = half * 4 + q
                            TR(ps[b][:, q * 128:(q + 1) * 128], fin[:, kc, q4 * 128:(q4 + 1) * 128], ident_f[:], [("fin", kc), "ident_f"], [("ps", b)])
                        eng = "vector" if half == 0 else "scalar"
                        CP(eng, ost[s][:, half * 512:(half + 1) * 512], ps[b][:], [("ps", b)], [("ost", s)])
                    DMA("sync", out_d[tile * 128:(tile + 1) * 128, :], ost[s][:], [("ost", s)], [("outd", tile)], f"o{s}")
        p.add("sync", lambda e: e.nop(nofuse=True), reads=["outd"])
        p.emit()
        return nc

    arena_reset()
    load_tokens()
    modulation_head(0)
    p.barrier()
    ffn(0, 0, 0, ST_ALL, side=(0, list(range(6, 18)), (24, 72)))
    if stage <= 1:
        return finish(False)
    even_mixer(0)
    if stage <= 2:
        return finish(False)
    ffn(0, 1, 6, ST_ALL, side=(1, list(range(0, 18)), (0, 72)), end_barrier=False)
    if stage <= 3:
        p.barrier()
        return finish(False)
    ffn(1, 0, 0, ST_ALL)
    if stage <= 4:
        return finish(False)
    odd_mixer(1)
    if stage <= 5:
        return finish(False)
    ffn(1, 1, 6, ST_X)
    if stage <= 6:
        return finish(False)
    return finish(True)


def make_inputs(inputs, b):
    f = lambda a: np.ascontiguousarray(a, dtype=np.float32)
    m = {}
    m["x"] = f(inputs["x"][b])
    m["ctx"] = f(inputs["ctx"][b])
    m["cvec"] = f(np.concatenate([np.asarray(inputs["c"][b]).reshape(8, 128), np.asarray(inputs["c_ctx"]).reshape(8, 128)], 0))
    return m


def shared_inputs(inputs):
    f = lambda a: np.ascontiguousarray(a, dtype=np.float32)
    m = {}
    m["ada_w"] = f(inputs["ada_w"])
    m["ada_b"] = f(np.asarray(inputs["ada_b"]).reshape(2, 72, 128))
    m["ffn_w_in"] = f(inputs["ffn_w_in"])
    m["ffn_w_out"] = f(inputs["ffn_w_out"])
    m["even_w_in"] = f(inputs["even_w_in"][0])
    m["even_w_out"] = f(inputs["even_w_out"][0])
    ewi = np.asarray(inputs["even_w_in"][0])
    m["even_wh"] = f(np.concatenate([ewi[:, j * 512 + h * 128:j * 512 + (h + 1) * 128] for h in range(4) for j in range(4)], 1))
    m["conv"] = f(np.asarray(inputs["mlstm_conv"][0]).reshape(24, 128))
    m["gate_b"] = f(np.asarray(inputs["mlstm_gate_b"][0]).reshape(1, 16))
    m["mnorm"] = f(np.asarray(inputs["mlstm_norm"][0]).reshape(1, 512))
    m["sgun"] = f(np.asarray(inputs["sgu_norm"][0]).reshape(1, 512))
    m["sgub"] = f(np.asarray(inputs["sgu_b"][0]).reshape(1, 512))
    m["sgu_ws"] = f(inputs["sgu_ws"][0])
    wqkv = np.asarray(inputs["odd_w_qkv"][0])
    perm = np.concatenate([np.arange(0, 64, 2), np.arange(1, 64, 2)])
    swp = np.concatenate([np.arange(1, 64, 2), np.arange(0, 64, 2)])
    qn = np.concatenate([wqkv[:, h * 64 + perm] for h in range(16)], 1)
    qs = np.concatenate([wqkv[:, h * 64 + swp] for h in range(16)], 1)
    m["odd_wq"] = f(np.concatenate([np.concatenate([qn[:, c * 128:(c + 1) * 128], qs[:, c * 128:(c + 1) * 128]], 1) for c in range(8)], 1))
    kn = np.concatenate([np.concatenate([wqkv[:, 1024 + g * 64 + perm]] * 2, 1) for g in range(4)], 1)
    ks = np.concatenate([np.concatenate([wqkv[:, 1024 + g * 64 + swp]] * 2, 1) for g in range(4)], 1)
    m["odd_wk"] = f(np.concatenate([np.concatenate([kn[:, g * 128:(g + 1) * 128], ks[:, g * 128:(g + 1) * 128]], 1) for g in range(4)], 1))
    m["odd_wv"] = f(wqkv[:, 1280:1536])
    m["odd_w_out"] = f(inputs["odd_w_out"][0])
    m["sink"] = f(np.asarray(inputs["attn_sink"][0]).reshape(1, 16))
    m["fnorm"] = f(np.asarray(inputs["final_norm"]).reshape(8, 128))
    tpos = np.arange(TX)
    row = (tpos // 64).astype(np.float32)
    col = (tpos % 64).astype(np.float32)
    inv = (10000.0 ** (-np.arange(16, dtype=np.float32) / 16)).astype(np.float32)
    ang = np.concatenate([row[:, None] * inv[None, :], col[:, None] * inv[None, :]], 1).astype(np.float32)
    cs, sn = np.cos(ang).astype(np.float32), np.sin(ang).astype(np.float32)
    pidx = np.arange(128)
    ropec = cs[:, pidx % 32].T
    sign = np.where((pidx % 64) < 32, -1.0, 1.0).astype(np.float32)
    ropes = sn[:, pidx % 32].T * sign[:, None]
    m["ropec"] = f(ropec)
    m["ropes"] = f(ropes)
    return m


_STAGE = 99


def kernel(**inputs):
    nc = build_program(_STAGE)
    shared = shared_inputs(inputs)
    in_maps = []
    for b in range(8):
        m = dict(shared)
        m.update(make_inputs(inputs, b))
        in_maps.append(m)
    res = run_bass_kernel_spmd(nc, in_maps, core_ids=list(range(8)))
    return np.stack([np.asarray(r["out"], dtype=np.float32) for r in res.results], 0)
```
